# Optimizing a Trainium2 kernel written in Bass

```python
import math
import jax, jax.numpy as jnp
from jax import lax
import numpy as np

D_MODEL = 1024
BATCH = 16
SEQ = 2048
DEPTH = 1

HEAD_DIM = 64
N_DIFF_HEADS = 4
DIFF_V_DIM = 2 * HEAD_DIM
N_DIL_HEADS = 8
DIL_PATTERNS = ((128, 1), (512, 4), (2048, 16))
Q_BLOCK = 128
D_FF = 2816
RMS_EPS = 1e-6
LAMBDA_STD = 0.1

DIFF_QK_WIDTH = N_DIFF_HEADS * 2 * HEAD_DIM
DIFF_V_WIDTH = N_DIFF_HEADS * DIFF_V_DIM
DIL_WIDTH = N_DIL_HEADS * HEAD_DIM
IN_PROJ_WIDTH = 2 * DIFF_QK_WIDTH + DIFF_V_WIDTH + 3 * DIL_WIDTH
MIX_WIDTH = DIFF_V_WIDTH + DIL_WIDTH
N_ALIBI_HEADS = N_DIFF_HEADS + N_DIL_HEADS

kernel_name = "hybrid_diffattn_dilated_macaron_layer"


def rms_norm(x, g):
    xf = x.astype(jnp.float32)
    y = xf * lax.rsqrt(jnp.mean(xf * xf, axis=-1, keepdims=True) + RMS_EPS)
    return (y * g.astype(jnp.float32)).astype(x.dtype)


def swiglu(x, w_gate, w_up, w_down):
    return (jax.nn.silu(x @ w_gate) * (x @ w_up)) @ w_down


def alibi_slopes():
    all_s = 2.0 ** (-8.0 * jnp.arange(1, N_ALIBI_HEADS + 1, dtype=jnp.float32) / N_ALIBI_HEADS)
    diff_idx = np.arange(0, N_ALIBI_HEADS, N_ALIBI_HEADS // N_DIFF_HEADS)
    dil_idx = np.setdiff1d(np.arange(N_ALIBI_HEADS), diff_idx)
    return all_s[diff_idx], all_s[dil_idx]


def lambda_init_fn(layer):
    return 0.8 - 0.6 * math.exp(-0.3 * layer)


def diff_attention(q, k, v, slopes, lam, lam_init, subln_g):
    B, S, H = q.shape[0], q.shape[1], q.shape[2]
    nb = S // Q_BLOCK
    scale = HEAD_DIM ** -0.5
    qb = q.reshape(B, nb, Q_BLOCK, H, 2, HEAD_DIM).transpose(1, 0, 2, 3, 4, 5)
    pos_k = jnp.arange(S)

    def block(args):
        qi, bi = args
        s = jnp.einsum('bqhcd,bkhcd->bhcqk', qi, k, preferred_element_type=jnp.float32) * scale
        pos_q = bi * Q_BLOCK + jnp.arange(Q_BLOCK)
        dist = pos_q[:, None] - pos_k[None, :]
        bias = -slopes[:, None, None] * dist.astype(jnp.float32)[None]
        s = s + bias[None, :, None]
        s = jnp.where((dist >= 0)[None, None, None], s, -jnp.inf)
        p = jax.nn.softmax(s, axis=-1)
        a = p[:, :, 0] - lam * p[:, :, 1]
        return jnp.einsum('bhqk,bkhd->bqhd', a.astype(v.dtype), v)

    o = lax.map(block, (qb, jnp.arange(nb)))
    o = o.transpose(1, 0, 2, 3, 4).reshape(B, S, H, DIFF_V_DIM)
    o = rms_norm(o, subln_g) * (1.0 - lam_init)
    return o.reshape(B, S, H * DIFF_V_DIM)


def dilated_pattern(q, k, v, slopes, window, dil):
    B, S, H, D = q.shape
    L = S // dil
    n_win = window // dil
    Lp = -(-L // Q_BLOCK) * Q_BLOCK
    nb = Lp // Q_BLOCK
    pad = Lp - L

    def to_blocks(a):
        a = a.reshape(B, L, dil, H, D).transpose(0, 2, 1, 3, 4)
        a = jnp.pad(a, ((0, 0), (0, 0), (0, pad), (0, 0), (0, 0)))
        return a.reshape(B, dil, nb, Q_BLOCK, H, D)

    def with_prev(a):
        prev = jnp.pad(a, ((0, 0), (0, 0), (1, 0), (0, 0), (0, 0), (0, 0)))[:, :, :-1]
        return jnp.concatenate([prev, a], axis=3)

    qb = to_blocks(q)
    kc = with_prev(to_blocks(k))
    vc = with_prev(to_blocks(v))
    s = jnp.einsum('brnqhd,brnkhd->brnhqk', qb, kc, preferred_element_type=jnp.float32) * (D ** -0.5)
    j = jnp.arange(Q_BLOCK)
    c = jnp.arange(2 * Q_BLOCK)
    dsub = (Q_BLOCK + j)[:, None] - c[None, :]
    key_idx = jnp.arange(nb)[:, None, None] * Q_BLOCK - Q_BLOCK + c[None, None, :]
    valid = (dsub >= 0)[None] & (dsub <= n_win)[None] & (key_idx >= 0)
    bias = -slopes[:, None, None] * (dsub * dil).astype(jnp.float32)[None]
    s = s + bias[None, None, None]
    s = jnp.where(valid[None, None, :, None], s, -jnp.inf)
    m = jnp.max(s, axis=-1, keepdims=True)
    e = jnp.exp(s - m)
    l = jnp.sum(e, axis=-1, keepdims=True)
    o = jnp.einsum('brnhqk,brnkhd->brnqhd', (e / l).astype(v.dtype), vc)
    lse = (m + jnp.log(l))[..., 0]
    o = o.reshape(B, dil, Lp, H, D)[:, :, :L].transpose(0, 2, 1, 3, 4).reshape(B, S, H, D)
    lse = lse.transpose(0, 1, 2, 4, 3).reshape(B, dil, Lp, H)[:, :, :L].transpose(0, 2, 1, 3).reshape(B, S, H)
    return o, lse


def dilated_attention(q, k, v, slopes):
    B, S, H, D = q.shape
    outs, lses = [], []
    for window, dil in DIL_PATTERNS:
        o, lse = dilated_pattern(q, k, v, slopes, window, dil)
        outs.append(o)
        lses.append(lse)
    w = jax.nn.softmax(jnp.stack(lses, axis=0), axis=0)
    o = sum(w[i][..., None] * outs[i].astype(jnp.float32) for i in range(len(DIL_PATTERNS)))
    return o.astype(q.dtype).reshape(B, S, H * D)


def hybrid_mixer(h, w_in, lam_q1, lam_k1, lam_q2, lam_k2, subln_g, w_out, layer):
    B, S, _ = h.shape
    proj = h @ w_in
    splits = np.cumsum([DIFF_QK_WIDTH, DIFF_QK_WIDTH, DIFF_V_WIDTH, DIL_WIDTH, DIL_WIDTH])
    q_d, k_d, v_d, q_l, k_l, v_l = jnp.split(proj, splits, axis=-1)
    slopes_d, slopes_l = alibi_slopes()
    lam_init = lambda_init_fn(layer)
    lam = (jnp.exp(jnp.sum(lam_q1.astype(jnp.float32) * lam_k1.astype(jnp.float32)))
           - jnp.exp(jnp.sum(lam_q2.astype(jnp.float32) * lam_k2.astype(jnp.float32))) + lam_init)
    o_d = diff_attention(q_d.reshape(B, S, N_DIFF_HEADS, 2, HEAD_DIM),
                         k_d.reshape(B, S, N_DIFF_HEADS, 2, HEAD_DIM),
                         v_d.reshape(B, S, N_DIFF_HEADS, DIFF_V_DIM),
                         slopes_d, lam, lam_init, subln_g)
    o_l = dilated_attention(q_l.reshape(B, S, N_DIL_HEADS, HEAD_DIM),
                            k_l.reshape(B, S, N_DIL_HEADS, HEAD_DIM),
                            v_l.reshape(B, S, N_DIL_HEADS, HEAD_DIM), slopes_l)
    return jnp.concatenate([o_d, o_l], axis=-1) @ w_out


def setup_inputs(seed: int = 0) -> dict:
    key = jax.random.key(seed)
    ks = jax.random.split(key, 24)
    f32 = jnp.float32

    def w(k, fan_in, fan_out):
        return jax.random.normal(k, (DEPTH, fan_in, fan_out), f32) * fan_in ** -0.5

    def gain(k, n):
        return 1.0 + 0.05 * jax.random.normal(k, (DEPTH, n), f32)

    return {
        "x": jax.random.normal(ks[0], (BATCH, SEQ, D_MODEL), f32),
        "ffn1_pre_g": gain(ks[1], D_MODEL),
        "ffn1_w_gate": w(ks[2], D_MODEL, D_FF),
        "ffn1_w_up": w(ks[3], D_MODEL, D_FF),
        "ffn1_w_down": w(ks[4], D_FF, D_MODEL),
        "ffn1_post_g": gain(ks[5], D_MODEL),
        "mix_pre_g": gain(ks[6], D_MODEL),
        "w_in": w(ks[7], D_MODEL, IN_PROJ_WIDTH),
        "lambda_q1": LAMBDA_STD * jax.random.normal(ks[8], (DEPTH, HEAD_DIM), f32),
        "lambda_k1": LAMBDA_STD * jax.random.normal(ks[9], (DEPTH, HEAD_DIM), f32),
        "lambda_q2": LAMBDA_STD * jax.random.normal(ks[10], (DEPTH, HEAD_DIM), f32),
        "lambda_k2": LAMBDA_STD * jax.random.normal(ks[11], (DEPTH, HEAD_DIM), f32),
        "diff_subln_g": gain(ks[12], DIFF_V_DIM),
        "w_out": w(ks[13], MIX_WIDTH, D_MODEL),
        "mix_post_g": gain(ks[14], D_MODEL),
        "ffn2_pre_g": gain(ks[15], D_MODEL),
        "ffn2_w_gate": w(ks[16], D_MODEL, D_FF),
        "ffn2_w_up": w(ks[17], D_MODEL, D_FF),
        "ffn2_w_down": w(ks[18], D_FF, D_MODEL),
        "ffn2_post_g": gain(ks[19], D_MODEL),
    }


def reference(x, ffn1_pre_g, ffn1_w_gate, ffn1_w_up, ffn1_w_down, ffn1_post_g,
              mix_pre_g, w_in, lambda_q1, lambda_k1, lambda_q2, lambda_k2, diff_subln_g,
              w_out, mix_post_g, ffn2_pre_g, ffn2_w_gate, ffn2_w_up, ffn2_w_down, ffn2_post_g):
    for l in range(DEPTH):
        h = swiglu(rms_norm(x, ffn1_pre_g[l]), ffn1_w_gate[l], ffn1_w_up[l], ffn1_w_down[l])
        x = x + 0.5 * rms_norm(h, ffn1_post_g[l])
        h = hybrid_mixer(rms_norm(x, mix_pre_g[l]), w_in[l], lambda_q1[l], lambda_k1[l],
                         lambda_q2[l], lambda_k2[l], diff_subln_g[l], w_out[l], l)
        x = x + rms_norm(h, mix_post_g[l])
        h = swiglu(rms_norm(x, ffn2_pre_g[l]), ffn2_w_gate[l], ffn2_w_up[l], ffn2_w_down[l])
        x = x + 0.5 * rms_norm(h, ffn2_post_g[l])
    return x
```

```python
import numpy as np
from contextlib import ExitStack
import concourse.bass as bass
import concourse.mybir as mybir
from concourse.bass_utils import run_bass_kernel_spmd

F32 = mybir.dt.float32
BF16 = mybir.dt.bfloat16
AF = mybir.ActivationFunctionType
ALU = mybir.AluOpType

PE, ACT, DVE, POOL, SP = "tensor", "scalar", "vector", "gpsimd", "sync"
ENGS = (PE, ACT, DVE, POOL, SP)

D = 1024
SEQ = 2048
DFF = 2816
NF = DFF // 128
NKC = D // 128
TT = 512
NTT = SEQ // TT
NSEQ = 2
EPS = 1e-6
SCALE = 0.125
N_SLOPE = 12
NOFF = 19
BIG = 3.0e38
LAM_INIT = 0.8 - 0.6 * 1.0
SUBLN_K = 1.0 - LAM_INIT

ALL_S = [2.0 ** (-8.0 * (i + 1) / 12.0) for i in range(12)]
DIFF_IDX = [0, 3, 6, 9]
DIL_IDX = [1, 2, 4, 5, 7, 8, 10, 11]


class Buf:
    __slots__ = ("name", "w", "r", "dsem", "dcnt")

    def __init__(self, name):
        self.name = name
        self.w = None
        self.r = {}
        self.dsem = None
        self.dcnt = 0


class Sched:
    def __init__(self, nc, stack):
        self.nc = nc
        self.stack = stack
        self.q = {e: [] for e in ENGS}
        self.cnt = {e: 0 for e in ENGS}
        self.seen = {e: {} for e in ENGS}
        self.sems = {}
        for e in (PE, ACT, DVE, POOL):
            self.sems[e] = stack.enter_context(nc.semaphore("s_" + e))
        self.nd = 0

    def _dma_sem(self, buf):
        if buf.dsem is None:
            key = "d%d" % self.nd
            self.nd += 1
            self.sems[key] = self.stack.enter_context(self.nc.semaphore(key))
            buf.dsem = key
        return buf.dsem

    def _wait(self, eng, tokens):
        need = {}
        for t in tokens:
            if t is None:
                continue
            k, v = t
            if v > need.get(k, 0):
                need[k] = v
        for k, v in need.items():
            if self.seen[eng].get(k, 0) >= v:
                continue
            self.seen[eng][k] = v
            sem = self.sems[k]
            self.q[eng].append(lambda e, sem=sem, v=v: e.wait_ge(sem, v))

    def _deps(self, eng, reads, writes, is_dma=False):
        toks = []
        for b in reads:
            toks.append(b.w)
        same_ok = (eng == PE) and not is_dma
        for b in writes:
            if b.w is not None and (b.w[0] != eng or not same_ok):
                toks.append(b.w)
            for k, v in b.r.items():
                if k != eng or not same_ok:
                    toks.append((k, v))
        return toks

    def op(self, eng, fn, reads=(), writes=(), inc=True):
        self._wait(eng, self._deps(eng, reads, writes))
        if inc:
            self.cnt[eng] += 1
            sem = self.sems[eng]
            self.q[eng].append(lambda e, fn=fn, sem=sem: fn(e).then_inc(sem, 1))
            tok = (eng, self.cnt[eng])
        else:
            self.q[eng].append(lambda e, fn=fn: fn(e))
            tok = (eng, self.cnt[eng] + 1)
        for b in writes:
            b.w = tok
            b.r = {}
        for b in reads:
            if tok[1] > b.r.get(eng, 0):
                b.r[eng] = tok[1]
        return tok

    def dma(self, eng, out, in_, reads=(), writes=(), track=None):
        self._wait(eng, self._deps(eng, reads, writes, is_dma=True))
        tb = track if track is not None else (writes[0] if writes else reads[0])
        key = self._dma_sem(tb)
        tb.dcnt += 16
        val = tb.dcnt
        sem = self.sems[key]
        self.q[eng].append(lambda e, out=out, in_=in_, sem=sem:
                           e.dma_start(out=out, in_=in_).then_inc(sem, 16))
        tok = (key, val)
        for b in writes:
            b.w = tok
            b.r = {}
        for b in reads:
            if val > b.r.get(key, 0):
                b.r[key] = val
        return tok

    def fence(self, old, new):
        acc = {}
        for b in old:
            if b.w is not None:
                acc[b.w[0]] = max(acc.get(b.w[0], 0), b.w[1])
            for k, v in b.r.items():
                acc[k] = max(acc.get(k, 0), v)
        for b in new:
            b.w = None
            b.r = dict(acc)

    def wait_all(self, eng, bufs):
        toks = []
        for b in bufs:
            toks.append(b.w)
            toks.extend(b.r.items())
        self._wait(eng, toks)

    def emit(self):
        with self.nc.Block() as block:
            for e in ENGS:
                lst = self.q[e]
                if not lst:
                    continue

                def body(eng, lst=lst):
                    for th in lst:
                        th(eng)
                getattr(block, e)(body)


def _dil_count(d):
    c = np.zeros_like(d, dtype=np.float32)
    c += ((d >= 0) & (d <= 128))
    c += ((d >= 0) & (d % 4 == 0) & (d <= 512))
    c += ((d >= 0) & (d % 16 == 0) & (d <= 2048))
    return c


def _mask_variant_dil(mp):
    if mp <= -5:
        return 0
    return mp + 5


def _consts():
    i = np.arange(128)[:, None]
    j = np.arange(512)[None, :]
    masks = np.zeros((9, 128, 512), np.float32)
    for v, mp in enumerate([-5, -4, -3, -2, -1, 0, 1, 2, 3]):
        d = j - i - 128 * mp
        masks[v] = _dil_count(d)
    tri = (np.arange(128)[None, :] >= np.arange(128)[:, None]).astype(np.float32)
    btab = np.zeros((128, N_SLOPE * NOFF), np.float32)
    for s in range(N_SLOPE):
        for o in range(NOFF):
            mp = o - 15
            btab[:, s * NOFF + o] = ALL_S[s] * (128.0 * mp + np.arange(128))
    import ml_dtypes
    bf = ml_dtypes.bfloat16
    aug = np.zeros((N_SLOPE, 2, 4, SEQ), np.float32)
    jj = (np.arange(SEQ) % TT).astype(np.float32)
    ii = (np.arange(SEQ) % 128).astype(np.float64)
    for s in range(N_SLOPE):
        aug[s, 0, 0] = -ALL_S[s] * jj / SCALE
        aug[s, 0, 1:4] = 1.0
        v = ALL_S[s] * ii / SCALE
        hi = v.astype(np.float32).astype(bf).astype(np.float64)
        mid = (v - hi).astype(np.float32).astype(bf).astype(np.float64)
        lo = (v - hi - mid).astype(np.float32).astype(bf).astype(np.float64)
        aug[s, 1, 0] = 1.0
        aug[s, 1, 1] = hi
        aug[s, 1, 2] = mid
        aug[s, 1, 3] = lo
    ident = np.eye(128, dtype=np.float32)
    return masks, tri, btab, aug, ident


def build_program(nseq=NSEQ, do_ffn1=True, do_attn=True, do_ffn2=True, stage=99):
    nc = bass.Bass("TRN2", target_bir_lowering=False)
    dt_in = lambda name, shape: nc.dram_tensor(name, shape, F32, kind="ExternalInput").ap()
    xT_d = dt_in("xT", [NSEQ, D, SEQ])
    wgu_d = [dt_in("wgu1", [NF, 128, 2 * NKC * 128]), dt_in("wgu2", [NF, 128, 2 * NKC * 128])]
    wd_d = [dt_in("wd1", [NKC, 128, NF * 128]), dt_in("wd2", [NKC, 128, NF * 128])]
    win_d = dt_in("win", [24, 128, NKC * 128])
    wout_d = dt_in("wout", [NKC, 128, NKC * 128])
    gains_d = dt_in("c_gains", [128, 48])
    gsub_d = dt_in("c_gsub", [128, 128])
    lam_d = dt_in("c_lam", [128, 256])
    masks_d = dt_in("c_masks", [9, 128, 512])
    tri_d = dt_in("c_tri", [128, 128])
    btab_d = dt_in("c_btab", [128, N_SLOPE * NOFF])
    aug_d = dt_in("c_aug", [N_SLOPE, 2, 4, SEQ])
    ident_d = dt_in("c_ident", [128, 128])
    outT_d = nc.dram_tensor("outT", [NSEQ, D, SEQ], F32, kind="ExternalOutput").ap()

    with ExitStack() as st:
        S = Sched(nc, st)
        sb = lambda name, shape, dt: st.enter_context(nc.sbuf_tensor(name, shape, dt))

        xT = sb("xT_sb", [128, NKC, SEQ], F32)
        r1 = sb("r1", [128, NKC * SEQ], BF16)
        U = sb("U", [128, 28704], BF16)
        ysb = sb("ysb", [128, NKC, TT], F32)
        WSLOT_E = NF * 128
        NWS = 3
        wsl = [sb("wsl%d" % i, [128, WSLOT_E], BF16) for i in range(NWS)]
        masks = sb("masks", [128, 9, 512], BF16)
        tri = sb("tri", [128, 128], BF16)
        btab = sb("btab", [128, N_SLOPE * NOFF], F32)
        ident = sb("ident", [128, 128], BF16)
        ones_bf = sb("ones_bf", [128, 128], BF16)
        gains = sb("gains", [128, 48], F32)
        gsub = sb("gsub", [128, 128], F32)
        lamt = sb("lamt", [128, 8], F32)
        ssq = sb("ssq", [128, 8], F32)
        sq = [sb("sq%d" % i, [128, TT], BF16) for i in range(2)]
        rst = [sb("rst%d" % i, [128, TT], F32) for i in range(1)]
        sg = [sb("sg%d" % i, [128, TT], F32) for i in range(1)]
        lamv = sg[0]
        Pt = [sb("Pt%d" % i, [128, TT], BF16) for i in range(3)]
        o0 = sb("o0", [128, 4, 128], F32)
        ot = [sb("ot%d" % i, [128, 128], F32) for i in range(1)]
        sml = [sb("sml%d" % i, [128, 4], F32) for i in range(4)]

        xn_h = r1[:, 0:NKC * 1024].rearrange("p (k t) -> p k t", k=NKC)
        xn_f = r1[:, :].rearrange("p (k t) -> p k t", k=NKC)
        hT = U[:, 0:NF * 1024].rearrange("p (f t) -> p f t", f=NF)
        mixT = U[:, 0:NKC * SEQ].rearrange("p (k t) -> p k t", k=NKC)
        o_qk = NKC * SEQ
        qk = U[:, o_qk:o_qk + 4 * SEQ].rearrange("p (a t) -> p a t", a=4)
        o_v = o_qk + 4 * SEQ
        Vt = U[:, o_v:o_v + 16 * 130].rearrange("p (t c) -> p t c", t=16)
        o_ot = o_v + 16 * 130
        Otok = U[:, o_ot:o_ot + 16 * 128].rearrange("p (t c) -> p t c", t=16)
        assert o_ot + 16 * 128 <= 28704

        ps = [st.enter_context(nc.psum_tensor("ps%d" % i, [128, 512], F32)) for i in range(8)]
        B_ps = [Buf("ps%d" % i) for i in range(8)]
        SB4 = [0, 1, 2, 7]
        SB6 = [0, 1, 2, 7, 4, 6]

        B_xT = [[Buf("xT%d_%d" % (k, t)) for t in range(NTT)] for k in range(NKC)]
        B_xnh = [[Buf("xnh%d_%d" % (k, t)) for t in range(2)] for k in range(NKC)]
        B_xnf = [[Buf("xnf%d_%d" % (k, t)) for t in range(NTT)] for k in range(NKC)]
        B_ysb = [Buf("ysb%d" % k) for k in range(NKC)]
        B_hT = [[Buf("hT%d_%d" % (f, t)) for t in range(2)] for f in range(NF)]
        B_mix = [[Buf("mix%d_%d" % (k, t)) for t in range(NTT)] for k in range(NKC)]
        B_qk = [[Buf("qk%d_%d" % (a, t)) for t in range(NTT)] for a in range(4)]
        B_qkaug = [Buf("qkaug%d" % a) for a in range(4)]
        B_V = [Buf("V%d" % t) for t in range(4)]
        B_Vones = Buf("Vones")
        B_Otok = [Buf("Otok%d" % t) for t in range(16)]
        B_wsl = [Buf("wsl%d" % i) for i in range(NWS)]
        B_const = Buf("const")
        B_wo = Buf("wo")
        B_cpool = Buf("cpool")
        B_ones = Buf("ones")
        B_lamt = Buf("lamt")
        B_ssq = Buf("ssq")
        B_sq = [Buf("sq%d" % i) for i in range(2)]
        B_rst = [Buf("rst%d" % i) for i in range(2)]
        B_sg = [Buf("sg%d" % i) for i in range(2)]
        B_Pt = [Buf("Pt%d" % i) for i in range(3)]
        B_o0 = [Buf("o0_%d" % i) for i in range(4)]
        B_ot = [Buf("ot%d" % i) for i in range(2)]
        B_sml = [Buf("sml%d" % i) for i in range(4)]

        sgh = [sg[0][:, 0:256].bitcast(BF16), sg[0][:, 256:512].bitcast(BF16)]
        B_sgh = [Buf("sgh0"), Buf("sgh1")]
        PtR = [Pt[0], Pt[1], Pt[2], sq[0], sq[1], sgh[0], sgh[1]]
        B_PtR = [B_Pt[0], B_Pt[1], B_Pt[2], B_sq[0], B_sq[1], B_sgh[0], B_sgh[1]]
        region_r1 = {"cur": []}
        region_U = {"cur": []}

        def flat(x):
            out = []
            for e in x:
                if isinstance(e, (list, tuple)):
                    out.extend(flat(e))
                else:
                    out.append(e)
            return out

        def switch(region, new):
            new = flat(new)
            old = region["cur"]
            hist = region.setdefault("hist", {})
            for b in old:
                if b.w is not None:
                    hist[b.w[0]] = max(hist.get(b.w[0], 0), b.w[1])
                for k, v in b.r.items():
                    hist[k] = max(hist.get(k, 0), v)
            oldids = set(map(id, old))
            for b in new:
                if id(b) not in oldids:
                    b.w = None
                    b.r = dict(hist)
            region["cur"] = new

        rot = {"ps": 0, "ps3": 0, "ps4": 0, "ps6": 0, "pt7": 0, "st": 0, "ws": 0, "sq": 0, "rst": 0, "sg": 0, "pt": 0, "ot": 0, "sml": 0}

        def nxt(key, n):
            v = rot[key]
            rot[key] = (v + 1) % n
            return v

        if stage == -1:
            pass
        S.dma(SP, btab[:], btab_d, writes=[B_const])
        S.dma(SP, gains[:], gains_d, writes=[B_const])
        S.dma(SP, gsub[:], gsub_d, writes=[B_const])
        S.dma(SP, lamv[:, 0:256], lam_d, writes=[B_sg[0]])
        cpool_loaded = []

        def load_cpool():
            if cpool_loaded:
                return
            cpool_loaded.append(1)
            S.dma(POOL, ident[:], ident_d, writes=[B_cpool])
            for v in range(9):
                S.dma(POOL, masks[:, v, :], masks_d[v], writes=[B_cpool])
            S.dma(POOL, tri[:], tri_d, writes=[B_cpool])
        S.op(DVE, lambda e: e.memset(ones_bf[:], 1.0), writes=[B_ones])
        S.op(DVE, lambda e: e.tensor_tensor(lamv[:, 0:64], lamv[:, 0:64], lamv[:, 64:128], ALU.mult),
             reads=[B_sg[0], B_lamt], writes=[B_sg[0], B_lamt])
        S.op(DVE, lambda e: e.tensor_tensor(lamv[:, 128:192], lamv[:, 128:192], lamv[:, 192:256], ALU.mult),
             reads=[B_sg[0], B_lamt], writes=[B_sg[0], B_lamt])
        S.op(DVE, lambda e: e.reduce_sum(lamt[:, 0:1], lamv[:, 0:64], mybir.AxisListType.X),
             reads=[B_sg[0], B_lamt], writes=[B_sg[0], B_lamt])
        S.op(DVE, lambda e: e.reduce_sum(lamt[:, 1:2], lamv[:, 128:192], mybir.AxisListType.X),
             reads=[B_sg[0], B_lamt], writes=[B_sg[0], B_lamt])
        S.op(ACT, lambda e: e.activation(lamt[:, 2:4], lamt[:, 0:2], AF.Exp), reads=[B_sg[0], B_lamt], writes=[B_sg[0], B_lamt])
        S.op(DVE, lambda e: e.scalar_tensor_tensor(lamt[:, 4:5], lamt[:, 3:4], -LAM_INIT, lamt[:, 2:3],
                                                   ALU.add, ALU.subtract),
             reads=[B_sg[0], B_lamt], writes=[B_sg[0], B_lamt])
        neg_lam = lamt[:, 4:5]

        def load_w(dram_ap, nelem):
            i = nxt("ws", NWS)
            S.dma(POOL, wsl[i][:, 0:nelem], dram_ap, writes=[B_wsl[i]])
            return i

        def slot_loader(dram_ap, nelem):
            wi = load_w(dram_ap, nelem)
            return (lambda k, wi=wi: wsl[wi][:, k * 128:(k + 1) * 128]), [B_wsl[wi]]

        def rms_stats(src_list, read_bufs_list, scale, bias):
            pb = 5 + nxt("st", 2)
            for kc in range(NKC):
                si = nxt("sq", 2)
                S.op(ACT, lambda e, si=si, src=src_list[kc]: e.activation(sq[si][:], src, AF.Square),
                     reads=read_bufs_list[kc], writes=[B_sq[si]])
                S.op(PE, lambda e, si=si, pb=pb, kc=kc: e.matmul(ps[pb][:], ones_bf[:], sq[si][:],
                                                                  start=(kc == 0), stop=(kc == NKC - 1)),
                     reads=[B_sq[si], B_ones], writes=[B_ps[pb]], inc=True)
            ri = nxt("rst", 1)
            S.op(ACT, lambda e, ri=ri, pb=pb: e.activation(rst[ri][:], ps[pb][:], AF.Sqrt, bias=bias, scale=scale),
                 reads=[B_ps[pb]], writes=[B_rst[ri]])
            S.op(DVE, lambda e, ri=ri: e.reciprocal(rst[ri][:], rst[ri][:]),
                 reads=[B_rst[ri]], writes=[B_rst[ri]])
            return ri

        def pre_norm(tt, gcol, dst_ap_fn, dst_buf_fn):
            srcs = [xT[:, kc, tt * TT:(tt + 1) * TT] for kc in range(NKC)]
            ri = rms_stats(srcs, [[B_xT[kc][tt]] for kc in range(NKC)], 1.0 / D, EPS)
            for kc in range(NKC):
                S.op(DVE, lambda e, kc=kc, ri=ri: e.scalar_tensor_tensor(
                    dst_ap_fn(kc), xT[:, kc, tt * TT:(tt + 1) * TT], gains[:, gcol + kc:gcol + kc + 1],
                    rst[ri][:], ALU.mult, ALU.mult),
                    reads=[B_xT[kc][tt], B_rst[ri], B_const], writes=[dst_buf_fn(kc)])

        def proj_post(tt, w_loader, rhs_fn, rhs_bufs_fn, nk, gcol, half):
            pb_stat = 5 + nxt("st", 2)
            pend_stat = []
            for d in range(NKC):
                w_fn, w_bufs = w_loader(d)
                pb = nxt("ps", 5)
                for k in range(nk):
                    S.op(PE, lambda e, w_fn=w_fn, pb=pb, k=k: e.matmul(
                        ps[pb][:], w_fn(k), rhs_fn(k),
                        start=(k == 0), stop=(k == nk - 1)),
                        reads=w_bufs + rhs_bufs_fn(k), writes=[B_ps[pb]], inc=(k == nk - 1))
                S.op(ACT, lambda e, pb=pb, d=d: e.activation(ysb[:, d, :], ps[pb][:], AF.Identity),
                     reads=[B_ps[pb]], writes=[B_ysb[d]])
                si = nxt("sq", 2)
                S.op(ACT, lambda e, pb=pb, si=si: e.activation(sq[si][:], ps[pb][:], AF.Square),
                     reads=[B_ps[pb]], writes=[B_sq[si]])
                if pend_stat:
                    pend_stat.pop()()
                pend_stat.append(lambda si=si, d=d: S.op(
                    PE, lambda e: e.matmul(ps[pb_stat][:], ones_bf[:], sq[si][:],
                                           start=(d == 0), stop=(d == NKC - 1)),
                    reads=[B_sq[si], B_ones], writes=[B_ps[pb_stat]], inc=True))
            pend_stat.pop()()
            ri = nxt("rst", 1)
            k2 = 4.0 if half else 1.0
            S.op(ACT, lambda e, ri=ri: e.activation(rst[ri][:], ps[pb_stat][:], AF.Sqrt,
                                                    bias=k2 * EPS, scale=k2 / D),
                 reads=[B_ps[pb_stat]], writes=[B_rst[ri]])
            S.op(DVE, lambda e, ri=ri: e.reciprocal(rst[ri][:], rst[ri][:]),
                 reads=[B_rst[ri]], writes=[B_rst[ri]])
            for d in range(NKC):
                S.op(DVE, lambda e, d=d, ri=ri: e.tensor_tensor(ysb[:, d, :], ysb[:, d, :], rst[ri][:], ALU.mult),
                     reads=[B_ysb[d], B_rst[ri]], writes=[B_ysb[d]])
                S.op(DVE, lambda e, d=d: e.scalar_tensor_tensor(
                    xT[:, d, tt * TT:(tt + 1) * TT], ysb[:, d, :], gains[:, gcol + d:gcol + d + 1],
                    xT[:, d, tt * TT:(tt + 1) * TT], ALU.mult, ALU.add),
                    reads=[B_ysb[d], B_xT[d][tt], B_const], writes=[B_xT[d][tt]])

        def ffn(fi, after_tile=None):
            gpre = 0 if fi == 0 else 32
            gpost = 8 if fi == 0 else 40
            switch(region_r1, [B_xnh, B_ysb])
            switch(region_U, [B_hT])
            def do_pre(h):
                for tl in range(2):
                    tt = 2 * h + tl
                    pre_norm(tt, gpre,
                             lambda kc, tl=tl: xn_h[:, kc, tl * TT:(tl + 1) * TT],
                             lambda kc, tl=tl: B_xnh[kc][tl])
            do_pre(0)
            for h in range(2):
                for f in range(NF):
                    wi = load_w(wgu_d[fi][f], 2 * NKC * 128)
                    for tl in range(2):
                        pg = nxt("ps", 5)
                        pu = nxt("ps", 5)
                        for gu, pb in ((0, pg), (1, pu)):
                            for kc in range(NKC):
                                S.op(PE, lambda e, wi=wi, pb=pb, gu=gu, kc=kc, tl=tl: e.matmul(
                                    ps[pb][:], wsl[wi][:, (gu * NKC + kc) * 128:(gu * NKC + kc + 1) * 128],
                                    xn_h[:, kc, tl * TT:(tl + 1) * TT], start=(kc == 0), stop=(kc == NKC - 1)),
                                    reads=[B_wsl[wi], B_xnh[kc][tl]], writes=[B_ps[pb]], inc=(kc == NKC - 1))
                        gi = nxt("sg", 1)
                        S.op(ACT, lambda e, gi=gi, pg=pg: e.activation(sg[gi][:], ps[pg][:], AF.Silu),
                             reads=[B_ps[pg]], writes=[B_sg[gi]])
                        S.op(DVE, lambda e, gi=gi, pu=pu, f=f, tl=tl: e.tensor_tensor(
                            hT[:, f, tl * TT:(tl + 1) * TT], sg[gi][:], ps[pu][:], ALU.mult),
                            reads=[B_sg[gi], B_ps[pu]], writes=[B_hT[f][tl]])
                if h == 0:
                    do_pre(1)
                for tl in range(2):
                    tt = 2 * h + tl
                    proj_post(tt,
                              lambda d: slot_loader(wd_d[fi][d], NF * 128),
                              lambda k, tl=tl: hT[:, k, tl * TT:(tl + 1) * TT],
                              lambda k, tl=tl: [B_hT[k][tl]],
                              NF, gpost, True)
                    if after_tile is not None:
                        after_tile(tt)

        def attention():
            load_cpool()
            switch(region_r1, [B_xnf])
            switch(region_U, [B_mix, B_qk, B_qkaug, B_V, B_Vones, B_Otok])
            for a in range(4):
                S.op(POOL, lambda e, a=a: e.memset(qk[64:128, a, :], 0.0), writes=B_qk[a])
            for tt in range(NTT):
                pre_norm(tt, 16,
                         lambda kc, tt=tt: xn_f[:, kc, tt * TT:(tt + 1) * TT],
                         lambda kc, tt=tt: B_xnf[kc][tt])
            prev_tail = []
            cur = {"diff": True}

            def sbank():
                if cur["diff"]:
                    return SB4[nxt("ps4", 4)]
                return SB6[nxt("ps6", 6)]
            S.fence([B_sg[0]], B_sgh)
            proj_q = []
            uinfo = []
            for u in range(8):
                if u < 4:
                    uinfo.append(dict(is_diff=True, cc=(u, 4 + u, 8 + u), slopes=[DIFF_IDX[u], DIFF_IDX[u]], w=None))
                else:
                    j = u - 4
                    uinfo.append(dict(is_diff=False, cc=(12 + j, 16 + j, 20 + j),
                                      slopes=[DIL_IDX[2 * j], DIL_IDX[2 * j + 1]], w=None))

            def tile_pieces(u, tt):
                info = uinfo[u]
                isd = info["is_diff"]
                pieces = []

                def p_weights():
                    if info["w"] is None:
                        info["w"] = [load_w(win_d[c], NKC * 128) for c in info["cc"]]
                pieces.append(p_weights)

                def p_aug():
                    for m in range(2):
                        sl = info["slopes"][m]
                        S.dma(POOL, qk[64:68, m, tt * TT:(tt + 1) * TT], aug_d[sl, 0, :, tt * TT:(tt + 1) * TT],
                              writes=[B_qk[m][tt]])
                        S.dma(POOL, qk[64:68, 2 + m, tt * TT:(tt + 1) * TT], aug_d[sl, 1, :, tt * TT:(tt + 1) * TT],
                              writes=[B_qk[2 + m][tt]])
                    if isd:
                        S.op(DVE, lambda e: e.memset(Vt[:, 4 * tt:4 * tt + 4, 128:129], 1.0), writes=[B_V[tt]])
                    else:
                        S.op(DVE, lambda e: e.memset(Vt[:, 4 * tt:4 * tt + 4, 64:65], 1.0), writes=[B_V[tt]])
                        S.op(DVE, lambda e: e.memset(Vt[:, 4 * tt:4 * tt + 4, 129:130], 1.0), writes=[B_V[tt]])
                pieces.append(p_aug)

                def p_qk(which):
                    wi = info["w"][which]
                    pb = sbank()
                    for kc in range(NKC):
                        S.op(PE, lambda e, wi=wi, pb=pb, kc=kc: e.matmul(
                            ps[pb][:], wsl[wi][:, kc * 128:(kc + 1) * 128],
                            xn_f[:, kc, tt * TT:(tt + 1) * TT], start=(kc == 0), stop=(kc == NKC - 1)),
                            reads=[B_wsl[wi], B_xnf[kc][tt]], writes=[B_ps[pb]], inc=(kc == NKC - 1))
                    a0 = 2 * which
                    if isd:
                        S.op(DVE, lambda e, pb=pb, a0=a0: e.tensor_copy(
                            qk[0:64, a0, tt * TT:(tt + 1) * TT], ps[pb][0:64, :]),
                            reads=[B_ps[pb]], writes=[B_qk[a0][tt]])
                        S.op(DVE, lambda e, pb=pb, a0=a0: e.tensor_copy(
                            qk[0:64, a0 + 1, tt * TT:(tt + 1) * TT], ps[pb][64:128, :]),
                            reads=[B_ps[pb]], writes=[B_qk[a0 + 1][tt]])
                    else:
                        S.op(ACT, lambda e, pb=pb, a0=a0: e.activation(
                            qk[0:64, a0, tt * TT:(tt + 1) * TT], ps[pb][0:64, :], AF.Identity),
                            reads=[B_ps[pb]], writes=[B_qk[a0][tt]])
                        S.op(ACT, lambda e, pb=pb, a0=a0: e.activation(
                            qk[0:64, a0 + 1, tt * TT:(tt + 1) * TT], ps[pb][64:128, :], AF.Identity),
                            reads=[B_ps[pb]], writes=[B_qk[a0 + 1][tt]])
                pieces.append(lambda: p_qk(0))
                pieces.append(lambda: p_qk(1))
                vstate = {}

                def p_v(tq):
                    wv = info["w"][2]
                    if tq == 0:
                        vstate["pb"] = sbank()
                    pb = vstate["pb"]
                    t = 4 * tt + tq
                    for kc in range(NKC):
                        S.op(PE, lambda e, pb=pb, kc=kc, t=t, tq=tq, wv=wv: e.matmul(
                            ps[pb][:, tq * 128:(tq + 1) * 128], xn_f[:, kc, t * 128:(t + 1) * 128],
                            wsl[wv][:, kc * 128:(kc + 1) * 128], start=(kc == 0), stop=(kc == NKC - 1)),
                            reads=[B_wsl[wv], B_xnf[kc][tt]], writes=[B_ps[pb]],
                            inc=(kc == NKC - 1))
                    if tq == 3:
                        src = ps[pb][:, :].rearrange("p (a c) -> p a c", a=4)
                        if isd:
                            S.op(DVE, lambda e, src=src: e.tensor_copy(Vt[:, 4 * tt:4 * tt + 4, 0:128], src),
                                 reads=[B_ps[pb]], writes=[B_V[tt]])
                        else:
                            S.op(ACT, lambda e, src=src: e.activation(
                                Vt[:, 4 * tt:4 * tt + 4, 0:64], src[:, :, 0:64], AF.Identity),
                                reads=[B_ps[pb]], writes=[B_V[tt]])
                            S.op(ACT, lambda e, src=src: e.activation(
                                Vt[:, 4 * tt:4 * tt + 4, 65:129], src[:, :, 64:128], AF.Identity),
                                reads=[B_ps[pb]], writes=[B_V[tt]])
                pieces.append(lambda: [p_v(tq) for tq in range(4)])
                return pieces

            for pc in tile_pieces(0, 0):
                pc()
            for u in range(8):
                is_diff = uinfo[u]["is_diff"]
                slopes = uinfo[u]["slopes"]
                while proj_q:
                    proj_q.pop(0)()
                jobs = [(Q, m, kt) for Q in range(NTT) for m in range(2) for kt in range(4 * Q + 4)]
                LOOK = 4 if is_diff else 6
                cur["diff"] = is_diff
                deferred = []
                pend = {}

                def map_params(m):
                    if is_diff:
                        return 128, 0
                    return 64, 65 * m

                def acc_ap(m, uq):
                    dvv, _ = map_params(m)
                    w1 = dvv + 1
                    if is_diff:
                        bank = (3 if m == 0 else 5) + uq // 2
                        return ps[bank][:, (uq % 2) * w1:(uq % 2) * w1 + w1], B_ps[bank]
                    bank = 3 if m == 0 else 5
                    return ps[bank][:, uq * w1:(uq + 1) * w1], B_ps[bank]

                def stage_a(ji):
                    Q, m, kt = jobs[ji]
                    mp = kt - 4 * Q
                    sl = slopes[m]
                    c0 = max(mp, 0) * 128
                    pb = sbank()
                    S.op(PE, lambda e, pb=pb, kt=kt, Q=Q, m=m, c0=c0: e.matmul(
                        ps[pb][:, c0:TT], qk[:, 2 + m, kt * 128:(kt + 1) * 128],
                        qk[:, m, Q * TT + c0:(Q + 1) * TT], start=True, stop=True),
                        reads=[B_qk[2 + m][kt // 4], B_qk[m][Q]], writes=[B_ps[pb]])
                    pi = nxt("pt", 5) if is_diff else nxt("pt7", 7)
                    bimm = float(ALL_S[sl] * 128.0 * mp)
                    S.op(ACT, lambda e, pb=pb, pi=pi, bimm=bimm, c0=c0: e.activation(
                        PtR[pi][:, c0:TT], ps[pb][:, c0:TT], AF.Exp, bias=bimm, scale=SCALE),
                        reads=[B_ps[pb]], writes=[B_PtR[pi]])
                    if not is_diff:
                        mv = _mask_variant_dil(mp)
                        if mp >= 0:
                            S.op(DVE, lambda e, pi=pi, mv=mv, c0=c0: e.scalar_tensor_tensor(
                                PtR[pi][:, c0:TT], PtR[pi][:, c0:TT], BIG, masks[:, mv, c0:TT], ALU.min, ALU.mult),
                                reads=[B_PtR[pi], B_cpool], writes=[B_PtR[pi]])
                        else:
                            meng = POOL if (ji % 3 == 2) else DVE
                            S.op(meng, lambda e, pi=pi, mv=mv: e.tensor_tensor(
                                PtR[pi][:], PtR[pi][:], masks[:, mv, :], ALU.mult),
                                reads=[B_PtR[pi], B_cpool], writes=[B_PtR[pi]])
                    elif mp >= 0:
                        S.op(DVE, lambda e, pi=pi, mp=mp: e.scalar_tensor_tensor(
                            PtR[pi][:, mp * 128:(mp + 1) * 128], PtR[pi][:, mp * 128:(mp + 1) * 128],
                            BIG, tri[:], ALU.min, ALU.mult),
                            reads=[B_PtR[pi], B_cpool], writes=[B_PtR[pi]])
                    pend[ji] = pi

                def flush_map(m):
                    last = -1
                    for idx, (tag, th) in enumerate(deferred):
                        if tag == m:
                            last = idx
                    for _ in range(last + 1):
                        deferred.pop(0)[1]()

                def epilogue(Q, m):
                    dvv, _ = map_params(m)

                    def D(fn, reads, writes, eng=DVE):
                        deferred.append((m, lambda: S.op(eng, fn, reads=reads, writes=writes)))
                    sis = []
                    for uq in range(4):
                        a_ap, a_buf = acc_ap(m, uq)
                        t = 4 * Q + uq
                        si = nxt("sml", 4)
                        sis.append(si)
                        D(lambda e, si=si, a_ap=a_ap, dvv=dvv: e.reciprocal(sml[si][:, 0:1], a_ap[:, dvv:dvv + 1]),
                          [a_buf], [B_sml[si]])
                        if not is_diff:
                            D(lambda e, si=si, a_ap=a_ap, t=t, m=m: e.tensor_scalar(
                                Otok[:, t, m * 64:(m + 1) * 64], a_ap[:, 0:64], sml[si][:, 0:1], None, ALU.mult),
                              [a_buf, B_sml[si]], [B_Otok[t]])
                        elif m == 0:
                            D(lambda e, si=si, a_ap=a_ap, uq=uq: e.tensor_scalar(
                                o0[:, uq, :], a_ap[:, 0:128], sml[si][:, 0:1], None, ALU.mult),
                              [a_buf, B_sml[si]], [B_o0[uq]])
                        else:
                            D(lambda e, si=si: e.tensor_tensor(sml[si][:, 1:2], sml[si][:, 0:1], neg_lam, ALU.mult),
                              [B_sml[si], B_lamt], [B_sml[si]])
                            D(lambda e, si=si, a_ap=a_ap, uq=uq: e.scalar_tensor_tensor(
                                o0[:, uq, :], a_ap[:, 0:128], sml[si][:, 1:2], o0[:, uq, :], ALU.mult, ALU.add),
                              [a_buf, B_sml[si], B_o0[uq]], [B_o0[uq]])
                    if is_diff and m == 1:
                        for uq in range(4):
                            j = 0
                            D(lambda e, uq=uq, j=j: e.tensor_tensor(ot[j][:], o0[:, uq, :], o0[:, uq, :], ALU.mult),
                              [B_o0[uq]], [B_ot[j]])
                            D(lambda e, uq=uq, j=j: e.reduce_sum(ssq[:, uq:uq + 1], ot[j][:], mybir.AxisListType.X),
                              [B_ot[j], B_ssq], [B_ssq])
                        for _ in range(24):
                            deferred.append((m, lambda: None))
                        D(lambda e: e.activation(ssq[:, 4:8], ssq[:, 0:4], AF.Sqrt,
                                                 bias=EPS / (SUBLN_K ** 2), scale=1.0 / (128.0 * SUBLN_K ** 2)),
                          [B_ssq], [B_ssq], eng=ACT)
                        D(lambda e: e.reciprocal(ssq[:, 4:8], ssq[:, 4:8]), [B_ssq], [B_ssq])
                        for uq in range(4):
                            t = 4 * Q + uq
                            D(lambda e, uq=uq, t=t: e.scalar_tensor_tensor(
                                Otok[:, t, :], o0[:, uq, :], ssq[:, 4 + uq:5 + uq], gsub[:], ALU.mult, ALU.mult),
                              [B_o0[uq], B_ssq, B_const], [B_Otok[t]])

                def stage_b(ji):
                    Q, m, kt = jobs[ji]
                    pi = pend.pop(ji)
                    dvv, vcol0 = map_params(m)
                    w1 = dvv + 1
                    uqs = [uq for uq in range(4) if kt <= 4 * Q + uq]
                    if kt == 0:
                        flush_map(m)
                    for uq in uqs:
                        a_ap, a_buf = acc_ap(m, uq)
                        st_flag = (kt == 0 and (uq % 2 == 0 if is_diff else uq == 0))
                        S.op(PE, lambda e, a_ap=a_ap, pi=pi, uq=uq, kt=kt, Q=Q, vcol0=vcol0, w1=w1, st_flag=st_flag: e.matmul(
                            a_ap, PtR[pi][:, uq * 128:(uq + 1) * 128], Vt[:, kt, vcol0:vcol0 + w1],
                            start=st_flag, stop=(kt == 4 * Q + uq), skip_group_check=True),
                            reads=[B_PtR[pi], B_V[kt // 4]], writes=[a_buf], inc=(uq == uqs[-1]))
                    if kt == 4 * Q + 3:
                        epilogue(Q, m)

                for i in range(len(jobs) + LOOK):
                    if i < len(jobs):
                        Qi, mi, kti = jobs[i]
                        if mi == 0 and kti == 0:
                            while proj_q:
                                proj_q.pop(0)()
                        stage_a(i)
                        if mi == 0 and kti == 0 and Qi + 1 < NTT:
                            proj_q.extend(tile_pieces(u, Qi + 1))
                        if Qi == 3 and mi == 1 and kti == 4 + LOOK and u + 1 < 8:
                            proj_q.extend(tile_pieces(u + 1, 0))
                        if i == 6 and prev_tail:
                            prev_tail.pop()()
                        if proj_q and i >= 2:
                            proj_q.pop(0)()
                    if i - LOOK >= 0:
                        stage_b(i - LOOK)
                    for _ in range(3):
                        if deferred:
                            deferred.pop(0)[1]()
                while deferred:
                    deferred.pop(0)[1]()
                def unit_tail(u=u):
                    for t4 in range(4):
                        bk = sbank()
                        pv = ps[bk][:, 0:256].bitcast(BF16)
                        for tq in range(4):
                            t = 4 * t4 + tq
                            S.op(PE, lambda e, t=t, tq=tq, pv=pv: e.transpose(
                                pv[:, tq * 128:(tq + 1) * 128], Otok[:, t, :], ident[:]),
                                reads=[B_Otok[t], B_cpool], writes=[B_ps[bk]], inc=(tq == 3))
                        S.op(ACT, lambda e, t4=t4, u=u, pv=pv: e.activation(
                            mixT[:, u, t4 * TT:(t4 + 1) * TT], pv[:, 0:TT], AF.Identity),
                            reads=[B_ps[bk]], writes=[B_mix[u][t4]])
                prev_tail.append(unit_tail)
            while prev_tail:
                prev_tail.pop()()
            S.fence(B_sgh, [B_sg[0]])
            switch(region_r1, [B_ysb])
            wo = U[:, o_qk:o_qk + 4 * SEQ].rearrange("p (d e) -> p d e", d=NKC)
            for d in range(NKC):
                S.dma(POOL, wo[:, d, :], wout_d[d], writes=B_qk[d // 2])
            for tt in range(NTT):
                proj_post(tt,
                          lambda d: ((lambda k, d=d: wo[:, d, k * 128:(k + 1) * 128]),
                                     B_qk[d // 2]),
                          lambda k, tt=tt: mixT[:, k, tt * TT:(tt + 1) * TT],
                          lambda k, tt=tt: [B_mix[k][tt]],
                          NKC, 24, False)

        def load_x(s, tt):
            for kc in range(NKC):
                S.dma(SP, xT[:, kc, tt * TT:(tt + 1) * TT],
                      xT_d[s, kc * 128:(kc + 1) * 128, tt * TT:(tt + 1) * TT], writes=[B_xT[kc][tt]])

        def store_x(s, tt):
            for kc in range(NKC):
                S.dma(SP, outT_d[s, kc * 128:(kc + 1) * 128, tt * TT:(tt + 1) * TT],
                      xT[:, kc, tt * TT:(tt + 1) * TT], reads=[B_xT[kc][tt]])

        for tt in range(NTT):
            load_x(0, tt)
        for s in range(nseq):
            if do_ffn1:
                ffn(0)
            if do_attn:
                attention()

            def tile_done(tt, s=s):
                store_x(s, tt)
                if s + 1 < nseq:
                    load_x(s + 1, tt)
            if do_ffn2:
                ffn(1, after_tile=tile_done)
            else:
                for tt in range(NTT):
                    tile_done(tt)
        S.wait_all(SP, flat(B_xT))
        S.emit()
    return nc


def _prep_weights(inp):
    f32 = np.float32
    out = {}

    def gu(wg, wu):
        a = np.stack([wg, wu], axis=0).reshape(2, NKC, 128, NF, 128)
        return np.ascontiguousarray(a.transpose(3, 2, 0, 1, 4)).reshape(NF, 128, 2 * NKC * 128)

    def dn(wd):
        a = wd.reshape(NF, 128, NKC, 128)
        return np.ascontiguousarray(a.transpose(2, 1, 0, 3)).reshape(NKC, 128, NF * 128)

    out["wgu1"] = gu(inp["ffn1_w_gate"][0], inp["ffn1_w_up"][0]).astype(f32, copy=False)
    out["wgu2"] = gu(inp["ffn2_w_gate"][0], inp["ffn2_w_up"][0]).astype(f32, copy=False)
    out["wd1"] = dn(inp["ffn1_w_down"][0])
    out["wd2"] = dn(inp["ffn2_w_down"][0])
    win = inp["w_in"][0].reshape(NKC, 128, 24, 128)
    out["win"] = np.ascontiguousarray(win.transpose(2, 1, 0, 3)).reshape(24, 128, NKC * 128)
    wout = inp["w_out"][0].reshape(NKC, 128, NKC, 128)
    out["wout"] = np.ascontiguousarray(wout.transpose(2, 1, 0, 3)).reshape(NKC, 128, NKC * 128)
    gl = [inp[k][0] for k in ("ffn1_pre_g", "ffn1_post_g", "mix_pre_g", "mix_post_g", "ffn2_pre_g", "ffn2_post_g")]
    out["c_gains"] = np.ascontiguousarray(
        np.concatenate([g.reshape(NKC, 128).T for g in gl], axis=1)).astype(f32)
    out["c_gsub"] = np.ascontiguousarray(np.broadcast_to(inp["diff_subln_g"][0][None, :], (128, 128))).astype(f32)
    lam = np.concatenate([inp["lambda_q1"][0], inp["lambda_k1"][0], inp["lambda_q2"][0], inp["lambda_k2"][0]])
    out["c_lam"] = np.ascontiguousarray(np.broadcast_to(lam[None, :], (128, 256))).astype(f32)
    masks, tri, btab, aug, ident = _consts()
    out["c_tri"] = tri
    out["c_masks"] = masks
    out["c_btab"] = btab
    out["c_aug"] = aug
    out["c_ident"] = ident
    return out


def kernel(**inputs):
    inp = {k: np.asarray(v) for k, v in inputs.items()}
    x = inp["x"]
    shared = _prep_weights(inp)
    nc = build_program()
    in_maps = []
    for c in range(8):
        m = dict(shared)
        m["xT"] = np.ascontiguousarray(x[NSEQ * c:NSEQ * (c + 1)].transpose(0, 2, 1))
        in_maps.append(m)
    res = run_bass_kernel_spmd(nc, in_maps, core_ids=list(range(8)))
    out = np.empty_like(x)
    for c in range(8):
        out[NSEQ * c:NSEQ * (c + 1)] = np.asarray(res.results[c]["outT"]).transpose(0, 2, 1)
    return out
```

```python
import numpy as np
from contextlib import ExitStack
import concourse.bass as bass
import concourse.mybir as mybir
from concourse.bass_utils import run_bass_kernel_spmd

F32 = mybir.dt.float32
BF16 = mybir.dt.bfloat16
AF = mybir.ActivationFunctionType
ALU = mybir.AluOpType

PE, ACT, DVE, POOL, SP = "tensor", "scalar", "vector", "gpsimd", "sync"
ENGS = (PE, ACT, DVE, POOL, SP)

D = 1024
SEQ = 2048
DFF = 2816
NF = DFF // 128
NKC = D // 128
TT = 512
NTT = SEQ // TT
NSEQ = 2
EPS = 1e-6
SCALE = 0.125
N_SLOPE = 12
NOFF = 19
BIG = 3.0e38
LAM_INIT = 0.8 - 0.6 * 1.0
SUBLN_K = 1.0 - LAM_INIT

ALL_S = [2.0 ** (-8.0 * (i + 1) / 12.0) for i in range(12)]
DIFF_IDX = [0, 3, 6, 9]
DIL_IDX = [1, 2, 4, 5, 7, 8, 10, 11]


class Buf:
    __slots__ = ("name", "w", "r", "dsem", "dcnt")

    def __init__(self, name):
        self.name = name
        self.w = None
        self.r = {}
        self.dsem = None
        self.dcnt = 0


class Sched:
    def __init__(self, nc, stack):
        self.nc = nc
        self.stack = stack
        self.q = {e: [] for e in ENGS}
        self.cnt = {e: 0 for e in ENGS}
        self.seen = {e: {} for e in ENGS}
        self.sems = {}
        for e in (PE, ACT, DVE, POOL):
            self.sems[e] = stack.enter_context(nc.semaphore("s_" + e))
        self.nd = 0

    def _dma_sem(self, buf):
        if buf.dsem is None:
            key = "d%d" % self.nd
            self.nd += 1
            self.sems[key] = self.stack.enter_context(self.nc.semaphore(key))
            buf.dsem = key
        return buf.dsem

    def _wait(self, eng, tokens):
        need = {}
        for t in tokens:
            if t is None:
                continue
            k, v = t
            if v > need.get(k, 0):
                need[k] = v
        for k, v in need.items():
            if self.seen[eng].get(k, 0) >= v:
                continue
            self.seen[eng][k] = v
            sem = self.sems[k]
            self.q[eng].append(lambda e, sem=sem, v=v: e.wait_ge(sem, v))

    def _deps(self, eng, reads, writes, is_dma=False):
        toks = []
        for b in reads:
            toks.append(b.w)
        same_ok = (eng == PE) and not is_dma
        for b in writes:
            if b.w is not None and (b.w[0] != eng or not same_ok):
                toks.append(b.w)
            for k, v in b.r.items():
                if k != eng or not same_ok:
                    toks.append((k, v))
        return toks

    def op(self, eng, fn, reads=(), writes=(), inc=True):
        self._wait(eng, self._deps(eng, reads, writes))
        if inc:
            self.cnt[eng] += 1
            sem = self.sems[eng]
            self.q[eng].append(lambda e, fn=fn, sem=sem: fn(e).then_inc(sem, 1))
            tok = (eng, self.cnt[eng])
        else:
            self.q[eng].append(lambda e, fn=fn: fn(e))
            tok = (eng, self.cnt[eng] + 1)
        for b in writes:
            b.w = tok
            b.r = {}
        for b in reads:
            if tok[1] > b.r.get(eng, 0):
                b.r[eng] = tok[1]
        return tok

    def dma(self, eng, out, in_, reads=(), writes=(), track=None):
        self._wait(eng, self._deps(eng, reads, writes, is_dma=True))
        tb = track if track is not None else (writes[0] if writes else reads[0])
        key = self._dma_sem(tb)
        tb.dcnt += 16
        val = tb.dcnt
        sem = self.sems[key]
        self.q[eng].append(lambda e, out=out, in_=in_, sem=sem:
                           e.dma_start(out=out, in_=in_).then_inc(sem, 16))
        tok = (key, val)
        for b in writes:
            b.w = tok
            b.r = {}
        for b in reads:
            if val > b.r.get(key, 0):
                b.r[key] = val
        return tok

    def fence(self, old, new):
        acc = {}
        for b in old:
            if b.w is not None:
                acc[b.w[0]] = max(acc.get(b.w[0], 0), b.w[1])
            for k, v in b.r.items():
                acc[k] = max(acc.get(k, 0), v)
        for b in new:
            b.w = None
            b.r = dict(acc)

    def wait_all(self, eng, bufs):
        toks = []
        for b in bufs:
            toks.append(b.w)
            toks.extend(b.r.items())
        self._wait(eng, toks)

    def emit(self):
        with self.nc.Block() as block:
            for e in ENGS:
                lst = self.q[e]
                if not lst:
                    continue

                def body(eng, lst=lst):
                    for th in lst:
                        th(eng)
                getattr(block, e)(body)


def _dil_count(d):
    c = np.zeros_like(d, dtype=np.float32)
    c += ((d >= 0) & (d <= 128))
    c += ((d >= 0) & (d % 4 == 0) & (d <= 512))
    c += ((d >= 0) & (d % 16 == 0) & (d <= 2048))
    return c


def _mask_variant_dil(mp):
    if mp <= -5:
        return 0
    return mp + 5


def _consts():
    i = np.arange(128)[:, None]
    j = np.arange(512)[None, :]
    masks = np.zeros((9, 128, 512), np.float32)
    for v, mp in enumerate([-5, -4, -3, -2, -1, 0, 1, 2, 3]):
        d = j - i - 128 * mp
        masks[v] = _dil_count(d)
    tri = (np.arange(128)[None, :] >= np.arange(128)[:, None]).astype(np.float32)
    btab = np.zeros((128, N_SLOPE * NOFF), np.float32)
    for s in range(N_SLOPE):
        for o in range(NOFF):
            mp = o - 15
            btab[:, s * NOFF + o] = ALL_S[s] * (128.0 * mp + np.arange(128))
    import ml_dtypes
    bf = ml_dtypes.bfloat16
    aug = np.zeros((N_SLOPE, 2, 4, SEQ), np.float32)
    jj = (np.arange(SEQ) % TT).astype(np.float32)
    ii = (np.arange(SEQ) % 128).astype(np.float64)
    for s in range(N_SLOPE):
        aug[s, 0, 0] = -ALL_S[s] * jj / SCALE
        aug[s, 0, 1:4] = 1.0
        v = ALL_S[s] * ii / SCALE
        hi = v.astype(np.float32).astype(bf).astype(np.float64)
        mid = (v - hi).astype(np.float32).astype(bf).astype(np.float64)
        lo = (v - hi - mid).astype(np.float32).astype(bf).astype(np.float64)
        aug[s, 1, 0] = 1.0
        aug[s, 1, 1] = hi
        aug[s, 1, 2] = mid
        aug[s, 1, 3] = lo
    ident = np.eye(128, dtype=np.float32)
    return masks, tri, btab, aug, ident


def build_program(nseq=NSEQ, do_ffn1=True, do_attn=True, do_ffn2=True, stage=99):
    nc = bass.Bass("TRN2", target_bir_lowering=False)
    dt_in = lambda name, shape: nc.dram_tensor(name, shape, F32, kind="ExternalInput").ap()
    xT_d = dt_in("xT", [NSEQ, D, SEQ])
    wgu_d = [dt_in("wgu1", [NF, 128, 2 * NKC * 128]), dt_in("wgu2", [NF, 128, 2 * NKC * 128])]
    wd_d = [dt_in("wd1", [NKC, 128, NF * 128]), dt_in("wd2", [NKC, 128, NF * 128])]
    win_d = dt_in("win", [24, 128, NKC * 128])
    wout_d = dt_in("wout", [NKC, 128, NKC * 128])
    gains_d = dt_in("c_gains", [128, 48])
    gsub_d = dt_in("c_gsub", [128, 128])
    lam_d = dt_in("c_lam", [128, 256])
    masks_d = dt_in("c_masks", [9, 128, 512])
    tri_d = dt_in("c_tri", [128, 128])
    btab_d = dt_in("c_btab", [128, N_SLOPE * NOFF])
    aug_d = dt_in("c_aug", [N_SLOPE, 2, 4, SEQ])
    ident_d = dt_in("c_ident", [128, 128])
    outT_d = nc.dram_tensor("outT", [NSEQ, D, SEQ], F32, kind="ExternalOutput").ap()

    with ExitStack() as st:
        S = Sched(nc, st)
        sb = lambda name, shape, dt: st.enter_context(nc.sbuf_tensor(name, shape, dt))

        xT = sb("xT_sb", [128, NKC, SEQ], F32)
        r1 = sb("r1", [128, NKC * SEQ], BF16)
        U = sb("U", [128, 28704], BF16)
        ysb = sb("ysb", [128, NKC, TT], F32)
        WSLOT_E = NF * 128
        NWS = 3
        wsl = [sb("wsl%d" % i, [128, WSLOT_E], BF16) for i in range(NWS)]
        masks = sb("masks", [128, 9, 512], BF16)
        tri = sb("tri", [128, 128], BF16)
        btab = sb("btab", [128, N_SLOPE * NOFF], F32)
        ident = sb("ident", [128, 128], BF16)
        ones_bf = sb("ones_bf", [128, 128], BF16)
        gains = sb("gains", [128, 48], F32)
        gsub = sb("gsub", [128, 128], F32)
        lamt = sb("lamt", [128, 8], F32)
        ssq = sb("ssq", [128, 8], F32)
        sq = [sb("sq%d" % i, [128, TT], BF16) for i in range(2)]
        rst = [sb("rst%d" % i, [128, TT], F32) for i in range(1)]
        sg = [sb("sg%d" % i, [128, TT], F32) for i in range(1)]
        lamv = sg[0]
        Pt = [sb("Pt%d" % i, [128, TT], BF16) for i in range(3)]
        o0 = sb("o0", [128, 4, 128], F32)
        ot = [sb("ot%d" % i, [128, 128], F32) for i in range(1)]
        sml = [sb("sml%d" % i, [128, 4], F32) for i in range(4)]

        xn_h = r1[:, 0:NKC * 1024].rearrange("p (k t) -> p k t", k=NKC)
        xn_f = r1[:, :].rearrange("p (k t) -> p k t", k=NKC)
        hT = U[:, 0:NF * 1024].rearrange("p (f t) -> p f t", f=NF)
        mixT = U[:, 0:NKC * SEQ].rearrange("p (k t) -> p k t", k=NKC)
        o_qk = NKC * SEQ
        qk = U[:, o_qk:o_qk + 4 * SEQ].rearrange("p (a t) -> p a t", a=4)
        o_v = o_qk + 4 * SEQ
        Vt = U[:, o_v:o_v + 16 * 130].rearrange("p (t c) -> p t c", t=16)
        o_ot = o_v + 16 * 130
        Otok = U[:, o_ot:o_ot + 16 * 128].rearrange("p (t c) -> p t c", t=16)
        assert o_ot + 16 * 128 <= 28704

        ps = [st.enter_context(nc.psum_tensor("ps%d" % i, [128, 512], F32)) for i in range(8)]
        B_ps = [Buf("ps%d" % i) for i in range(8)]
        SB4 = [0, 1, 2, 7]
        SB6 = [0, 1, 2, 7, 4, 6]

        B_xT = [[Buf("xT%d_%d" % (k, t)) for t in range(NTT)] for k in range(NKC)]
        B_xnh = [[Buf("xnh%d_%d" % (k, t)) for t in range(2)] for k in range(NKC)]
        B_xnf = [[Buf("xnf%d_%d" % (k, t)) for t in range(NTT)] for k in range(NKC)]
        B_ysb = [Buf("ysb%d" % k) for k in range(NKC)]
        B_hT = [[Buf("hT%d_%d" % (f, t)) for t in range(2)] for f in range(NF)]
        B_mix = [[Buf("mix%d_%d" % (k, t)) for t in range(NTT)] for k in range(NKC)]
        B_qk = [[Buf("qk%d_%d" % (a, t)) for t in range(NTT)] for a in range(4)]
        B_qkaug = [Buf("qkaug%d" % a) for a in range(4)]
        B_V = [Buf("V%d" % t) for t in range(4)]
        B_Vones = Buf("Vones")
        B_Otok = [Buf("Otok%d" % t) for t in range(16)]
        B_wsl = [Buf("wsl%d" % i) for i in range(NWS)]
        B_const = Buf("const")
        B_wo = Buf("wo")
        B_cpool = Buf("cpool")
        B_ones = Buf("ones")
        B_lamt = Buf("lamt")
        B_ssq = Buf("ssq")
        B_sq = [Buf("sq%d" % i) for i in range(2)]
        B_rst = [Buf("rst%d" % i) for i in range(2)]
        B_sg = [Buf("sg%d" % i) for i in range(2)]
        B_Pt = [Buf("Pt%d" % i) for i in range(3)]
        B_o0 = [Buf("o0_%d" % i) for i in range(4)]
        B_ot = [Buf("ot%d" % i) for i in range(2)]
        B_sml = [Buf("sml%d" % i) for i in range(4)]

        sgh = [sg[0][:, 0:256].bitcast(BF16), sg[0][:, 256:512].bitcast(BF16)]
        B_sgh = [Buf("sgh0"), Buf("sgh1")]
        PtR = [Pt[0], Pt[1], Pt[2], sq[0], sq[1], sgh[0], sgh[1]]
        B_PtR = [B_Pt[0], B_Pt[1], B_Pt[2], B_sq[0], B_sq[1], B_sgh[0], B_sgh[1]]
        region_r1 = {"cur": []}
        region_U = {"cur": []}

        def flat(x):
            out = []
            for e in x:
                if isinstance(e, (list, tuple)):
                    out.extend(flat(e))
                else:
                    out.append(e)
            return out

        def switch(region, new):
            new = flat(new)
            old = region["cur"]
            hist = region.setdefault("hist", {})
            for b in old:
                if b.w is not None:
                    hist[b.w[0]] = max(hist.get(b.w[0], 0), b.w[1])
                for k, v in b.r.items():
                    hist[k] = max(hist.get(k, 0), v)
            oldids = set(map(id, old))
            for b in new:
                if id(b) not in oldids:
                    b.w = None
                    b.r = dict(hist)
            region["cur"] = new

        rot = {"ps": 0, "ps3": 0, "ps4": 0, "ps6": 0, "pt7": 0, "st": 0, "ws": 0, "sq": 0, "rst": 0, "sg": 0, "pt": 0, "ot": 0, "sml": 0}

        def nxt(key, n):
            v = rot[key]
            rot[key] = (v + 1) % n
            return v

        if stage == -1:
            pass
        S.dma(SP, btab[:], btab_d, writes=[B_const])
        S.dma(SP, gains[:], gains_d, writes=[B_const])
        S.dma(SP, gsub[:], gsub_d, writes=[B_const])
        S.dma(SP, lamv[:, 0:256], lam_d, writes=[B_sg[0]])
        cpool_loaded = []

        def load_cpool():
            if cpool_loaded:
                return
            cpool_loaded.append(1)
            S.dma(POOL, ident[:], ident_d, writes=[B_cpool])
            for v in range(9):
                S.dma(POOL, masks[:, v, :], masks_d[v], writes=[B_cpool])
            S.dma(POOL, tri[:], tri_d, writes=[B_cpool])
        S.op(DVE, lambda e: e.memset(ones_bf[:], 1.0), writes=[B_ones])
        S.op(DVE, lambda e: e.tensor_tensor(lamv[:, 0:64], lamv[:, 0:64], lamv[:, 64:128], ALU.mult),
             reads=[B_sg[0], B_lamt], writes=[B_sg[0], B_lamt])
        S.op(DVE, lambda e: e.tensor_tensor(lamv[:, 128:192], lamv[:, 128:192], lamv[:, 192:256], ALU.mult),
             reads=[B_sg[0], B_lamt], writes=[B_sg[0], B_lamt])
        S.op(DVE, lambda e: e.reduce_sum(lamt[:, 0:1], lamv[:, 0:64], mybir.AxisListType.X),
             reads=[B_sg[0], B_lamt], writes=[B_sg[0], B_lamt])
        S.op(DVE, lambda e: e.reduce_sum(lamt[:, 1:2], lamv[:, 128:192], mybir.AxisListType.X),
             reads=[B_sg[0], B_lamt], writes=[B_sg[0], B_lamt])
        S.op(ACT, lambda e: e.activation(lamt[:, 2:4], lamt[:, 0:2], AF.Exp), reads=[B_sg[0], B_lamt], writes=[B_sg[0], B_lamt])
        S.op(DVE, lambda e: e.scalar_tensor_tensor(lamt[:, 4:5], lamt[:, 3:4], -LAM_INIT, lamt[:, 2:3],
                                                   ALU.add, ALU.subtract),
             reads=[B_sg[0], B_lamt], writes=[B_sg[0], B_lamt])
        neg_lam = lamt[:, 4:5]

        def load_w(dram_ap, nelem):
            i = nxt("ws", NWS)
            S.dma(POOL, wsl[i][:, 0:nelem], dram_ap, writes=[B_wsl[i]])
            return i

        def slot_loader(dram_ap, nelem):
            wi = load_w(dram_ap, nelem)
            return (lambda k, wi=wi: wsl[wi][:, k * 128:(k + 1) * 128]), [B_wsl[wi]]

        def rms_stats(src_list, read_bufs_list, scale, bias):
            pb = 5 + nxt("st", 2)
            for kc in range(NKC):
                si = nxt("sq", 2)
                S.op(ACT, lambda e, si=si, src=src_list[kc]: e.activation(sq[si][:], src, AF.Square),
                     reads=read_bufs_list[kc], writes=[B_sq[si]])
                S.op(PE, lambda e, si=si, pb=pb, kc=kc: e.matmul(ps[pb][:], ones_bf[:], sq[si][:],
                                                                  start=(kc == 0), stop=(kc == NKC - 1)),
                     reads=[B_sq[si], B_ones], writes=[B_ps[pb]], inc=True)
            ri = nxt("rst", 1)
            S.op(ACT, lambda e, ri=ri, pb=pb: e.activation(rst[ri][:], ps[pb][:], AF.Sqrt, bias=bias, scale=scale),
                 reads=[B_ps[pb]], writes=[B_rst[ri]])
            S.op(DVE, lambda e, ri=ri: e.reciprocal(rst[ri][:], rst[ri][:]),
                 reads=[B_rst[ri]], writes=[B_rst[ri]])
            return ri

        def stat_step(pb, kc, src, read_bufs):
            si = nxt("sq", 2)
            S.op(ACT, lambda e, si=si, src=src: e.activation(sq[si][:], src, AF.Square),
                 reads=read_bufs, writes=[B_sq[si]])
            S.op(PE, lambda e, si=si, pb=pb, kc=kc: e.matmul(ps[pb][:], ones_bf[:], sq[si][:],
                                                              start=(kc == 0), stop=(kc == NKC - 1)),
                 reads=[B_sq[si], B_ones], writes=[B_ps[pb]], inc=True)

        def stat_finish(pb, scale, bias):
            ri = nxt("rst", 1)
            S.op(ACT, lambda e, ri=ri, pb=pb: e.activation(rst[ri][:], ps[pb][:], AF.Sqrt, bias=bias, scale=scale),
                 reads=[B_ps[pb]], writes=[B_rst[ri]])
            S.op(DVE, lambda e, ri=ri: e.reciprocal(rst[ri][:], rst[ri][:]),
                 reads=[B_rst[ri]], writes=[B_rst[ri]])
            return ri

        def pre_apply(tt, ri, gcol, dst_ap_fn, dst_buf_fn):
            for kc in range(NKC):
                S.op(DVE, lambda e, kc=kc, ri=ri: e.scalar_tensor_tensor(
                    dst_ap_fn(kc), xT[:, kc, tt * TT:(tt + 1) * TT], gains[:, gcol + kc:gcol + kc + 1],
                    rst[ri][:], ALU.mult, ALU.mult),
                    reads=[B_xT[kc][tt], B_rst[ri], B_const], writes=[dst_buf_fn(kc)])

        def pre_norm(tt, gcol, dst_ap_fn, dst_buf_fn):
            srcs = [xT[:, kc, tt * TT:(tt + 1) * TT] for kc in range(NKC)]
            ri = rms_stats(srcs, [[B_xT[kc][tt]] for kc in range(NKC)], 1.0 / D, EPS)
            for kc in range(NKC):
                S.op(DVE, lambda e, kc=kc, ri=ri: e.scalar_tensor_tensor(
                    dst_ap_fn(kc), xT[:, kc, tt * TT:(tt + 1) * TT], gains[:, gcol + kc:gcol + kc + 1],
                    rst[ri][:], ALU.mult, ALU.mult),
                    reads=[B_xT[kc][tt], B_rst[ri], B_const], writes=[dst_buf_fn(kc)])

        def proj_post(tt, w_loader, rhs_fn, rhs_bufs_fn, nk, gcol, half):
            pb_stat = 5 + nxt("st", 2)
            pend_stat = []
            for d in range(NKC):
                w_fn, w_bufs = w_loader(d)
                pb = nxt("ps", 5)
                for k in range(nk):
                    S.op(PE, lambda e, w_fn=w_fn, pb=pb, k=k: e.matmul(
                        ps[pb][:], w_fn(k), rhs_fn(k),
                        start=(k == 0), stop=(k == nk - 1)),
                        reads=w_bufs + rhs_bufs_fn(k), writes=[B_ps[pb]], inc=(k == nk - 1))
                S.op(ACT, lambda e, pb=pb, d=d: e.activation(ysb[:, d, :], ps[pb][:], AF.Identity),
                     reads=[B_ps[pb]], writes=[B_ysb[d]])
                si = nxt("sq", 2)
                S.op(ACT, lambda e, pb=pb, si=si: e.activation(sq[si][:], ps[pb][:], AF.Square),
                     reads=[B_ps[pb]], writes=[B_sq[si]])
                if pend_stat:
                    pend_stat.pop()()
                pend_stat.append(lambda si=si, d=d: S.op(
                    PE, lambda e: e.matmul(ps[pb_stat][:], ones_bf[:], sq[si][:],
                                           start=(d == 0), stop=(d == NKC - 1)),
                    reads=[B_sq[si], B_ones], writes=[B_ps[pb_stat]], inc=True))
            pend_stat.pop()()
            ri = nxt("rst", 1)
            k2 = 4.0 if half else 1.0
            S.op(ACT, lambda e, ri=ri: e.activation(rst[ri][:], ps[pb_stat][:], AF.Sqrt,
                                                    bias=k2 * EPS, scale=k2 / D),
                 reads=[B_ps[pb_stat]], writes=[B_rst[ri]])
            S.op(DVE, lambda e, ri=ri: e.reciprocal(rst[ri][:], rst[ri][:]),
                 reads=[B_rst[ri]], writes=[B_rst[ri]])
            for d in range(NKC):
                S.op(DVE, lambda e, d=d, ri=ri: e.tensor_tensor(ysb[:, d, :], ysb[:, d, :], rst[ri][:], ALU.mult),
                     reads=[B_ysb[d], B_rst[ri]], writes=[B_ysb[d]])
                S.op(DVE, lambda e, d=d: e.scalar_tensor_tensor(
                    xT[:, d, tt * TT:(tt + 1) * TT], ysb[:, d, :], gains[:, gcol + d:gcol + d + 1],
                    xT[:, d, tt * TT:(tt + 1) * TT], ALU.mult, ALU.add),
                    reads=[B_ysb[d], B_xT[d][tt], B_const], writes=[B_xT[d][tt]])

        def ffn(fi, after_tile=None):
            gpre = 0 if fi == 0 else 32
            gpost = 8 if fi == 0 else 40
            switch(region_r1, [B_xnh, B_ysb])
            switch(region_U, [B_hT])
            def do_pre(h):
                for tl in range(2):
                    tt = 2 * h + tl
                    pre_norm(tt, gpre,
                             lambda kc, tl=tl: xn_h[:, kc, tl * TT:(tl + 1) * TT],
                             lambda kc, tl=tl: B_xnh[kc][tl])
            do_pre(0)
            for h in range(2):
                if h == 0:
                    nx_pb = [5 + nxt("st", 2), 5 + nxt("st", 2)]
                    nx_steps = [(tlp, kc) for tlp in range(2) for kc in range(NKC)]
                for f in range(NF):
                    wi = load_w(wgu_d[fi][f], 2 * NKC * 128)
                    for tl in range(2):
                        pg = nxt("ps", 5)
                        pu = nxt("ps", 5)
                        for gu, pb in ((0, pg), (1, pu)):
                            for kc in range(NKC):
                                S.op(PE, lambda e, wi=wi, pb=pb, gu=gu, kc=kc, tl=tl: e.matmul(
                                    ps[pb][:], wsl[wi][:, (gu * NKC + kc) * 128:(gu * NKC + kc + 1) * 128],
                                    xn_h[:, kc, tl * TT:(tl + 1) * TT], start=(kc == 0), stop=(kc == NKC - 1)),
                                    reads=[B_wsl[wi], B_xnh[kc][tl]], writes=[B_ps[pb]], inc=(kc == NKC - 1))
                        gi = nxt("sg", 1)
                        S.op(ACT, lambda e, gi=gi, pg=pg: e.activation(sg[gi][:], ps[pg][:], AF.Silu),
                             reads=[B_ps[pg]], writes=[B_sg[gi]])
                        S.op(DVE, lambda e, gi=gi, pu=pu, f=f, tl=tl: e.tensor_tensor(
                            hT[:, f, tl * TT:(tl + 1) * TT], sg[gi][:], ps[pu][:], ALU.mult),
                            reads=[B_sg[gi], B_ps[pu]], writes=[B_hT[f][tl]])
                        if h == 0 and (2 * f + tl) >= 2 * NF - len(nx_steps) - 2 and nx_steps:
                            tlp, kc = nx_steps.pop(0)
                            stat_step(nx_pb[tlp], kc, xT[:, kc, (2 + tlp) * TT:(3 + tlp) * TT], [B_xT[kc][2 + tlp]])
                if h == 0:
                    while nx_steps:
                        tlp, kc = nx_steps.pop(0)
                        stat_step(nx_pb[tlp], kc, xT[:, kc, (2 + tlp) * TT:(3 + tlp) * TT], [B_xT[kc][2 + tlp]])
                    for tlp in range(2):
                        ri = stat_finish(nx_pb[tlp], 1.0 / D, EPS)
                        pre_apply(2 + tlp, ri, gpre,
                                  lambda kc, tlp=tlp: xn_h[:, kc, tlp * TT:(tlp + 1) * TT],
                                  lambda kc, tlp=tlp: B_xnh[kc][tlp])
                for tl in range(2):
                    tt = 2 * h + tl
                    proj_post(tt,
                              lambda d: slot_loader(wd_d[fi][d], NF * 128),
                              lambda k, tl=tl: hT[:, k, tl * TT:(tl + 1) * TT],
                              lambda k, tl=tl: [B_hT[k][tl]],
                              NF, gpost, True)
                    if after_tile is not None:
                        after_tile(tt)

        def attention():
            load_cpool()
            switch(region_r1, [B_xnf])
            switch(region_U, [B_mix, B_qk, B_qkaug, B_V, B_Vones, B_Otok])
            for a in range(4):
                S.op(POOL, lambda e, a=a: e.memset(qk[64:128, a, :], 0.0), writes=B_qk[a])
            for tt in range(NTT):
                pre_norm(tt, 16,
                         lambda kc, tt=tt: xn_f[:, kc, tt * TT:(tt + 1) * TT],
                         lambda kc, tt=tt: B_xnf[kc][tt])
            prev_tail = []
            cur = {"diff": True}

            def sbank():
                if cur["diff"]:
                    return SB4[nxt("ps4", 4)]
                return SB6[nxt("ps6", 6)]
            S.fence([B_sg[0]], B_sgh)
            proj_q = []
            uinfo = []
            for u in range(8):
                if u < 4:
                    uinfo.append(dict(is_diff=True, cc=(u, 4 + u, 8 + u), slopes=[DIFF_IDX[u], DIFF_IDX[u]], w=None))
                else:
                    j = u - 4
                    uinfo.append(dict(is_diff=False, cc=(12 + j, 16 + j, 20 + j),
                                      slopes=[DIL_IDX[2 * j], DIL_IDX[2 * j + 1]], w=None))

            def tile_pieces(u, tt):
                info = uinfo[u]
                isd = info["is_diff"]
                pieces = []

                def p_weights():
                    if info["w"] is None:
                        info["w"] = [load_w(win_d[c], NKC * 128) for c in info["cc"]]
                pieces.append(p_weights)

                def p_aug():
                    for m in range(2):
                        sl = info["slopes"][m]
                        S.dma(POOL, qk[64:68, m, tt * TT:(tt + 1) * TT], aug_d[sl, 0, :, tt * TT:(tt + 1) * TT],
                              writes=[B_qk[m][tt]])
                        S.dma(POOL, qk[64:68, 2 + m, tt * TT:(tt + 1) * TT], aug_d[sl, 1, :, tt * TT:(tt + 1) * TT],
                              writes=[B_qk[2 + m][tt]])
                    if isd:
                        S.op(DVE, lambda e: e.memset(Vt[:, 4 * tt:4 * tt + 4, 128:129], 1.0), writes=[B_V[tt]])
                    else:
                        S.op(DVE, lambda e: e.memset(Vt[:, 4 * tt:4 * tt + 4, 64:65], 1.0), writes=[B_V[tt]])
                        S.op(DVE, lambda e: e.memset(Vt[:, 4 * tt:4 * tt + 4, 129:130], 1.0), writes=[B_V[tt]])
                pieces.append(p_aug)

                def p_qk(which):
                    wi = info["w"][which]
                    pb = sbank()
                    for kc in range(NKC):
                        S.op(PE, lambda e, wi=wi, pb=pb, kc=kc: e.matmul(
                            ps[pb][:], wsl[wi][:, kc * 128:(kc + 1) * 128],
                            xn_f[:, kc, tt * TT:(tt + 1) * TT], start=(kc == 0), stop=(kc == NKC - 1)),
                            reads=[B_wsl[wi], B_xnf[kc][tt]], writes=[B_ps[pb]], inc=(kc == NKC - 1))
                    a0 = 2 * which
                    if isd:
                        S.op(DVE, lambda e, pb=pb, a0=a0: e.tensor_copy(
                            qk[0:64, a0, tt * TT:(tt + 1) * TT], ps[pb][0:64, :]),
                            reads=[B_ps[pb]], writes=[B_qk[a0][tt]])
                        S.op(DVE, lambda e, pb=pb, a0=a0: e.tensor_copy(
                            qk[0:64, a0 + 1, tt * TT:(tt + 1) * TT], ps[pb][64:128, :]),
                            reads=[B_ps[pb]], writes=[B_qk[a0 + 1][tt]])
                    else:
                        S.op(ACT, lambda e, pb=pb, a0=a0: e.activation(
                            qk[0:64, a0, tt * TT:(tt + 1) * TT], ps[pb][0:64, :], AF.Identity),
                            reads=[B_ps[pb]], writes=[B_qk[a0][tt]])
                        S.op(ACT, lambda e, pb=pb, a0=a0: e.activation(
                            qk[0:64, a0 + 1, tt * TT:(tt + 1) * TT], ps[pb][64:128, :], AF.Identity),
                            reads=[B_ps[pb]], writes=[B_qk[a0 + 1][tt]])
                pieces.append(lambda: p_qk(0))
                pieces.append(lambda: p_qk(1))
                vstate = {}

                def p_v(tq):
                    wv = info["w"][2]
                    if tq == 0:
                        vstate["pb"] = sbank()
                    pb = vstate["pb"]
                    t = 4 * tt + tq
                    for kc in range(NKC):
                        S.op(PE, lambda e, pb=pb, kc=kc, t=t, tq=tq, wv=wv: e.matmul(
                            ps[pb][:, tq * 128:(tq + 1) * 128], xn_f[:, kc, t * 128:(t + 1) * 128],
                            wsl[wv][:, kc * 128:(kc + 1) * 128], start=(kc == 0), stop=(kc == NKC - 1)),
                            reads=[B_wsl[wv], B_xnf[kc][tt]], writes=[B_ps[pb]],
                            inc=(kc == NKC - 1))
                    if tq == 3:
                        src = ps[pb][:, :].rearrange("p (a c) -> p a c", a=4)
                        if isd:
                            S.op(DVE, lambda e, src=src: e.tensor_copy(Vt[:, 4 * tt:4 * tt + 4, 0:128], src),
                                 reads=[B_ps[pb]], writes=[B_V[tt]])
                        else:
                            S.op(ACT, lambda e, src=src: e.activation(
                                Vt[:, 4 * tt:4 * tt + 4, 0:64], src[:, :, 0:64], AF.Identity),
                                reads=[B_ps[pb]], writes=[B_V[tt]])
                            S.op(ACT, lambda e, src=src: e.activation(
                                Vt[:, 4 * tt:4 * tt + 4, 65:129], src[:, :, 64:128], AF.Identity),
                                reads=[B_ps[pb]], writes=[B_V[tt]])
                pieces.append(lambda: [p_v(tq) for tq in range(4)])
                return pieces

            for pc in tile_pieces(0, 0):
                pc()
            for u in range(8):
                is_diff = uinfo[u]["is_diff"]
                slopes = uinfo[u]["slopes"]
                while proj_q:
                    proj_q.pop(0)()
                jobs = [(Q, m, kt) for Q in range(NTT) for m in range(2) for kt in range(4 * Q + 4)]
                LOOK = 4 if is_diff else 6
                cur["diff"] = is_diff
                deferred = []
                pend = {}

                def map_params(m):
                    if is_diff:
                        return 128, 0
                    return 64, 65 * m

                def acc_ap(m, uq):
                    dvv, _ = map_params(m)
                    w1 = dvv + 1
                    if is_diff:
                        bank = (3 if m == 0 else 5) + uq // 2
                        return ps[bank][:, (uq % 2) * w1:(uq % 2) * w1 + w1], B_ps[bank]
                    bank = 3 if m == 0 else 5
                    return ps[bank][:, uq * w1:(uq + 1) * w1], B_ps[bank]

                def stage_a(ji):
                    Q, m, kt = jobs[ji]
                    mp = kt - 4 * Q
                    sl = slopes[m]
                    c0 = max(mp, 0) * 128
                    pb = sbank()
                    S.op(PE, lambda e, pb=pb, kt=kt, Q=Q, m=m, c0=c0: e.matmul(
                        ps[pb][:, c0:TT], qk[:, 2 + m, kt * 128:(kt + 1) * 128],
                        qk[:, m, Q * TT + c0:(Q + 1) * TT], start=True, stop=True),
                        reads=[B_qk[2 + m][kt // 4], B_qk[m][Q]], writes=[B_ps[pb]])
                    pi = nxt("pt", 5) if is_diff else nxt("pt7", 7)
                    bimm = float(ALL_S[sl] * 128.0 * mp)
                    S.op(ACT, lambda e, pb=pb, pi=pi, bimm=bimm, c0=c0: e.activation(
                        PtR[pi][:, c0:TT], ps[pb][:, c0:TT], AF.Exp, bias=bimm, scale=SCALE),
                        reads=[B_ps[pb]], writes=[B_PtR[pi]])
                    if not is_diff:
                        mv = _mask_variant_dil(mp)
                        if mp >= 0:
                            S.op(DVE, lambda e, pi=pi, mv=mv, c0=c0: e.scalar_tensor_tensor(
                                PtR[pi][:, c0:TT], PtR[pi][:, c0:TT], BIG, masks[:, mv, c0:TT], ALU.min, ALU.mult),
                                reads=[B_PtR[pi], B_cpool], writes=[B_PtR[pi]])
                        else:
                            meng = POOL if (ji % 3 == 2) else DVE
                            S.op(meng, lambda e, pi=pi, mv=mv: e.tensor_tensor(
                                PtR[pi][:], PtR[pi][:], masks[:, mv, :], ALU.mult),
                                reads=[B_PtR[pi], B_cpool], writes=[B_PtR[pi]])
                    elif mp >= 0:
                        S.op(DVE, lambda e, pi=pi, mp=mp: e.scalar_tensor_tensor(
                            PtR[pi][:, mp * 128:(mp + 1) * 128], PtR[pi][:, mp * 128:(mp + 1) * 128],
                            BIG, tri[:], ALU.min, ALU.mult),
                            reads=[B_PtR[pi], B_cpool], writes=[B_PtR[pi]])
                    pend[ji] = pi

                def flush_map(m):
                    last = -1
                    for idx, (tag, th) in enumerate(deferred):
                        if tag == m:
                            last = idx
                    for _ in range(last + 1):
                        deferred.pop(0)[1]()

                def epilogue(Q, m):
                    dvv, _ = map_params(m)

                    def D(fn, reads, writes, eng=DVE):
                        deferred.append((m, lambda: S.op(eng, fn, reads=reads, writes=writes)))
                    sis = []
                    for uq in range(4):
                        a_ap, a_buf = acc_ap(m, uq)
                        t = 4 * Q + uq
                        si = nxt("sml", 4)
                        sis.append(si)
                        D(lambda e, si=si, a_ap=a_ap, dvv=dvv: e.reciprocal(sml[si][:, 0:1], a_ap[:, dvv:dvv + 1]),
                          [a_buf], [B_sml[si]])
                        if not is_diff:
                            D(lambda e, si=si, a_ap=a_ap, t=t, m=m: e.tensor_scalar(
                                Otok[:, t, m * 64:(m + 1) * 64], a_ap[:, 0:64], sml[si][:, 0:1], None, ALU.mult),
                              [a_buf, B_sml[si]], [B_Otok[t]])
                        elif m == 0:
                            D(lambda e, si=si, a_ap=a_ap, uq=uq: e.tensor_scalar(
                                o0[:, uq, :], a_ap[:, 0:128], sml[si][:, 0:1], None, ALU.mult),
                              [a_buf, B_sml[si]], [B_o0[uq]])
                        else:
                            D(lambda e, si=si: e.tensor_tensor(sml[si][:, 1:2], sml[si][:, 0:1], neg_lam, ALU.mult),
                              [B_sml[si], B_lamt], [B_sml[si]])
                            D(lambda e, si=si, a_ap=a_ap, uq=uq: e.scalar_tensor_tensor(
                                o0[:, uq, :], a_ap[:, 0:128], sml[si][:, 1:2], o0[:, uq, :], ALU.mult, ALU.add),
                              [a_buf, B_sml[si], B_o0[uq]], [B_o0[uq]])
                    if is_diff and m == 1:
                        for uq in range(4):
                            j = 0
                            D(lambda e, uq=uq, j=j: e.tensor_tensor(ot[j][:], o0[:, uq, :], o0[:, uq, :], ALU.mult),
                              [B_o0[uq]], [B_ot[j]])
                            D(lambda e, uq=uq, j=j: e.reduce_sum(ssq[:, uq:uq + 1], ot[j][:], mybir.AxisListType.X),
                              [B_ot[j], B_ssq], [B_ssq])
                        for _ in range(24):
                            deferred.append((m, lambda: None))
                        D(lambda e: e.activation(ssq[:, 4:8], ssq[:, 0:4], AF.Sqrt,
                                                 bias=EPS / (SUBLN_K ** 2), scale=1.0 / (128.0 * SUBLN_K ** 2)),
                          [B_ssq], [B_ssq], eng=ACT)
                        D(lambda e: e.reciprocal(ssq[:, 4:8], ssq[:, 4:8]), [B_ssq], [B_ssq])
                        for uq in range(4):
                            t = 4 * Q + uq
                            D(lambda e, uq=uq, t=t: e.scalar_tensor_tensor(
                                Otok[:, t, :], o0[:, uq, :], ssq[:, 4 + uq:5 + uq], gsub[:], ALU.mult, ALU.mult),
                              [B_o0[uq], B_ssq, B_const], [B_Otok[t]])

                def stage_b(ji):
                    Q, m, kt = jobs[ji]
                    pi = pend.pop(ji)
                    dvv, vcol0 = map_params(m)
                    w1 = dvv + 1
                    uqs = [uq for uq in range(4) if kt <= 4 * Q + uq]
                    if kt == 0:
                        flush_map(m)
                    for uq in uqs:
                        a_ap, a_buf = acc_ap(m, uq)
                        st_flag = (kt == 0 and (uq % 2 == 0 if is_diff else uq == 0))
                        S.op(PE, lambda e, a_ap=a_ap, pi=pi, uq=uq, kt=kt, Q=Q, vcol0=vcol0, w1=w1, st_flag=st_flag: e.matmul(
                            a_ap, PtR[pi][:, uq * 128:(uq + 1) * 128], Vt[:, kt, vcol0:vcol0 + w1],
                            start=st_flag, stop=(kt == 4 * Q + uq), skip_group_check=True),
                            reads=[B_PtR[pi], B_V[kt // 4]], writes=[a_buf], inc=(uq == uqs[-1]))
                    if kt == 4 * Q + 3:
                        epilogue(Q, m)

                for i in range(len(jobs) + LOOK):
                    if i < len(jobs):
                        Qi, mi, kti = jobs[i]
                        if mi == 0 and kti == 0:
                            while proj_q:
                                proj_q.pop(0)()
                        stage_a(i)
                        if mi == 0 and kti == 0 and Qi + 1 < NTT:
                            proj_q.extend(tile_pieces(u, Qi + 1))
                        if Qi == 3 and mi == 1 and kti == 4 + LOOK and u + 1 < 8:
                            proj_q.extend(tile_pieces(u + 1, 0))
                        if i == 6 and prev_tail:
                            prev_tail.pop()()
                        if proj_q and i >= 2:
                            proj_q.pop(0)()
                    if i - LOOK >= 0:
                        stage_b(i - LOOK)
                    for _ in range(3):
                        if deferred:
                            deferred.pop(0)[1]()
                while deferred:
                    deferred.pop(0)[1]()
                def unit_tail(u=u):
                    for t4 in range(4):
                        bk = sbank()
                        pv = ps[bk][:, 0:256].bitcast(BF16)
                        for tq in range(4):
                            t = 4 * t4 + tq
                            S.op(PE, lambda e, t=t, tq=tq, pv=pv: e.transpose(
                                pv[:, tq * 128:(tq + 1) * 128], Otok[:, t, :], ident[:]),
                                reads=[B_Otok[t], B_cpool], writes=[B_ps[bk]], inc=(tq == 3))
                        S.op(ACT, lambda e, t4=t4, u=u, pv=pv: e.activation(
                            mixT[:, u, t4 * TT:(t4 + 1) * TT], pv[:, 0:TT], AF.Identity),
                            reads=[B_ps[bk]], writes=[B_mix[u][t4]])
                prev_tail.append(unit_tail)
            while prev_tail:
                prev_tail.pop()()
            S.fence(B_sgh, [B_sg[0]])
            switch(region_r1, [B_ysb])
            wo = U[:, o_qk:o_qk + 4 * SEQ].rearrange("p (d e) -> p d e", d=NKC)
            for d in range(NKC):
                S.dma(POOL, wo[:, d, :], wout_d[d], writes=B_qk[d // 2])
            for tt in range(NTT):
                proj_post(tt,
                          lambda d: ((lambda k, d=d: wo[:, d, k * 128:(k + 1) * 128]),
                                     B_qk[d // 2]),
                          lambda k, tt=tt: mixT[:, k, tt * TT:(tt + 1) * TT],
                          lambda k, tt=tt: [B_mix[k][tt]],
                          NKC, 24, False)

        def load_x(s, tt):
            for kc in range(NKC):
                S.dma(SP, xT[:, kc, tt * TT:(tt + 1) * TT],
                      xT_d[s, kc * 128:(kc + 1) * 128, tt * TT:(tt + 1) * TT], writes=[B_xT[kc][tt]])

        def store_x(s, tt):
            for kc in range(NKC):
                S.dma(SP, outT_d[s, kc * 128:(kc + 1) * 128, tt * TT:(tt + 1) * TT],
                      xT[:, kc, tt * TT:(tt + 1) * TT], reads=[B_xT[kc][tt]])

        for tt in range(NTT):
            load_x(0, tt)
        for s in range(nseq):
            if do_ffn1:
                ffn(0)
            if do_attn:
                attention()

            def tile_done(tt, s=s):
                store_x(s, tt)
                if s + 1 < nseq:
                    load_x(s + 1, tt)
            if do_ffn2:
                ffn(1, after_tile=tile_done)
            else:
                for tt in range(NTT):
                    tile_done(tt)
        S.wait_all(SP, flat(B_xT))
        S.emit()
    return nc


def _prep_weights(inp):
    f32 = np.float32
    out = {}

    def gu(wg, wu):
        a = np.stack([wg, wu], axis=0).reshape(2, NKC, 128, NF, 128)
        return np.ascontiguousarray(a.transpose(3, 2, 0, 1, 4)).reshape(NF, 128, 2 * NKC * 128)

    def dn(wd):
        a = wd.reshape(NF, 128, NKC, 128)
        return np.ascontiguousarray(a.transpose(2, 1, 0, 3)).reshape(NKC, 128, NF * 128)

    out["wgu1"] = gu(inp["ffn1_w_gate"][0], inp["ffn1_w_up"][0]).astype(f32, copy=False)
    out["wgu2"] = gu(inp["ffn2_w_gate"][0], inp["ffn2_w_up"][0]).astype(f32, copy=False)
    out["wd1"] = dn(inp["ffn1_w_down"][0])
    out["wd2"] = dn(inp["ffn2_w_down"][0])
    win = inp["w_in"][0].reshape(NKC, 128, 24, 128)
    out["win"] = np.ascontiguousarray(win.transpose(2, 1, 0, 3)).reshape(24, 128, NKC * 128)
    wout = inp["w_out"][0].reshape(NKC, 128, NKC, 128)
    out["wout"] = np.ascontiguousarray(wout.transpose(2, 1, 0, 3)).reshape(NKC, 128, NKC * 128)
    gl = [inp[k][0] for k in ("ffn1_pre_g", "ffn1_post_g", "mix_pre_g", "mix_post_g", "ffn2_pre_g", "ffn2_post_g")]
    out["c_gains"] = np.ascontiguousarray(
        np.concatenate([g.reshape(NKC, 128).T for g in gl], axis=1)).astype(f32)
    out["c_gsub"] = np.ascontiguousarray(np.broadcast_to(inp["diff_subln_g"][0][None, :], (128, 128))).astype(f32)
    lam = np.concatenate([inp["lambda_q1"][0], inp["lambda_k1"][0], inp["lambda_q2"][0], inp["lambda_k2"][0]])
    out["c_lam"] = np.ascontiguousarray(np.broadcast_to(lam[None, :], (128, 256))).astype(f32)
    masks, tri, btab, aug, ident = _consts()
    out["c_tri"] = tri
    out["c_masks"] = masks
    out["c_btab"] = btab
    out["c_aug"] = aug
    out["c_ident"] = ident
    return out


def kernel(**inputs):
    inp = {k: np.asarray(v) for k, v in inputs.items()}
    x = inp["x"]
    shared = _prep_weights(inp)
    nc = build_program()
    in_maps = []
    for c in range(8):
        m = dict(shared)
        m["xT"] = np.ascontiguousarray(x[NSEQ * c:NSEQ * (c + 1)].transpose(0, 2, 1))
        in_maps.append(m)
    res = run_bass_kernel_spmd(nc, in_maps, core_ids=list(range(8)))
    out = np.empty_like(x)
    for c in range(8):
        out[NSEQ * c:NSEQ * (c + 1)] = np.asarray(res.results[c]["outT"]).transpose(0, 2, 1)
    return out
```
